# Optimizing a Trainium2 kernel written in Bass

```python
import jax, jax.numpy as jnp
from jax import lax
import numpy as np

D_MODEL = 1024
BATCH = 16
SEQ = 4096
DEPTH = 1
DEC_BATCH = 8
DEC_SEQ = 32
PAST_LEN = 1024

CHUNK = 64
N_PAST_CHUNKS = 8
BAND = (N_PAST_CHUNKS + 1) * CHUNK
ATT_WINDOW = N_PAST_CHUNKS * CHUNK
D_CONV = D_MODEL // 2
D_ATTN = D_MODEL // 2
N_HEADS = 8
HEAD_DIM = D_ATTN // N_HEADS
CONV_WIDTH = 31
MAX_REL = 128
N_IN = 3 * D_CONV + 4 * D_ATTN + 2 * D_MODEL
EPS = 1e-6

kernel_name = "chunk_stream_conformer_hybrid_step"


def rms_norm(x, g):
    xf = x.astype(jnp.float32)
    y = xf * lax.rsqrt(jnp.mean(xf * xf, axis=-1, keepdims=True) + EPS)
    return (y * g.astype(jnp.float32)).astype(x.dtype)


def layer_norm(x, g, b):
    xf = x.astype(jnp.float32)
    mu = jnp.mean(xf, axis=-1, keepdims=True)
    var = jnp.mean(jnp.square(xf - mu), axis=-1, keepdims=True)
    y = (xf - mu) * lax.rsqrt(var + EPS)
    return (y * g.astype(jnp.float32) + b.astype(jnp.float32)).astype(x.dtype)


def layer_inputs(x, c, g_pre, w_mod, b_mod, w_in):
    bsz, L, _ = x.shape
    mod = c @ w_mod + b_mod
    shift, scale, gate = jnp.split(mod, 3, axis=-1)
    h = rms_norm(x, g_pre) * (1 + scale[:, None, :]) + shift[:, None, :]
    z = h @ w_in
    cuts = np.cumsum([D_CONV, D_CONV, D_CONV, D_ATTN, D_ATTN, D_ATTN, D_ATTN, D_MODEL])
    a, b, z_conv, q, k, v, z_attn, g_conv, g_attn = jnp.split(z, cuts, axis=-1)
    u = a * jax.nn.sigmoid(b)
    heads = lambda t: t.reshape(bsz, L, N_HEADS, HEAD_DIM)
    return u, z_conv, heads(q), heads(k), heads(v), z_attn, g_conv, g_attn, gate


def conv_branch(u_hist, z_conv, dw_w, dw_b, ln_g, ln_b, w_conv_out):
    y = lax.conv_general_dilated(u_hist, dw_w[:, None, :], window_strides=(1,), padding='VALID',
                                 dimension_numbers=('NWC', 'WIO', 'NWC'),
                                 feature_group_count=D_CONV) + dw_b
    y = jax.nn.silu(layer_norm(y, ln_g, ln_b)) * jax.nn.silu(z_conv)
    return y @ w_conv_out


def band_bias_mask(q_pos, k_pos, table):
    rel = q_pos[..., :, None] - k_pos[..., None, :]
    idx = jnp.clip(rel, -MAX_REL, MAX_REL) + MAX_REL
    bias = jnp.moveaxis(table.astype(jnp.float32)[:, idx], 0, -3)
    qc = (q_pos // CHUNK)[..., :, None]
    kc = (k_pos // CHUNK)[..., None, :]
    mask = (k_pos[..., None, :] >= 0) & (kc <= qc) & (kc >= qc - N_PAST_CHUNKS)
    return bias, mask[..., None, :, :]


def attend(q, k, v, bias, mask):
    s = jnp.einsum('...qhd,...khd->...hqk', q, k).astype(jnp.float32) * (HEAD_DIM ** -0.5) + bias
    s = jnp.where(mask, s, -1e30)
    p = jax.nn.softmax(s, axis=-1).astype(v.dtype)
    return jnp.einsum('...hqk,...khd->...qhd', p, v)


def prompt_band_attention(q, k, v, table):
    _, S, _, _ = q.shape
    n_chunks = S // CHUNK
    q_pos = jnp.arange(S).reshape(n_chunks, CHUNK)
    k_pos = (jnp.arange(n_chunks)[:, None] - N_PAST_CHUNKS) * CHUNK + jnp.arange(BAND)[None, :]
    bias, mask = band_bias_mask(q_pos, k_pos, table)
    band_idx = jnp.arange(n_chunks)[:, None] + jnp.arange(N_PAST_CHUNKS + 1)[None, :]

    def gather_band(t):
        pad = jnp.zeros((N_PAST_CHUNKS * CHUNK,) + t.shape[1:], t.dtype)
        tc = jnp.concatenate([pad, t], axis=0).reshape(n_chunks + N_PAST_CHUNKS, CHUNK, N_HEADS, HEAD_DIM)
        return tc[band_idx].reshape(n_chunks, BAND, N_HEADS, HEAD_DIM)

    def one_stream(args):
        qs, ks, vs = args
        o = attend(qs.reshape(n_chunks, CHUNK, N_HEADS, HEAD_DIM), gather_band(ks), gather_band(vs), bias, mask)
        return o.reshape(S, N_HEADS, HEAD_DIM)

    return lax.map(one_stream, (q, k, v))


def sample_band_attention(q, k_all, v_all, table):
    T = q.shape[1]
    W = k_all.shape[1] - T
    q_pos = PAST_LEN + jnp.arange(T)
    k_pos = PAST_LEN - W + jnp.arange(W + T)
    bias, mask = band_bias_mask(q_pos, k_pos, table)
    return attend(q, k_all, v_all, bias, mask)


def attn_branch_out(o, z_attn, w_attn_out):
    bsz, L = o.shape[:2]
    return (o.reshape(bsz, L, D_ATTN) * jax.nn.silu(z_attn)) @ w_attn_out


def layer_output(x, y_conv, y_attn, g_conv, g_attn, gate, w_o, g_post):
    merged = jax.nn.sigmoid(g_conv) * y_conv + jax.nn.sigmoid(g_attn) * y_attn
    out = rms_norm(merged @ w_o, g_post)
    return x + gate[:, None, :] * out


def setup_inputs(seed: int = 0) -> dict:
    key = jax.random.key(seed)
    ks = jax.random.split(key, 20)
    nrm = lambda k, shape, s: jax.random.normal(k, shape, jnp.float32) * s
    W = min(ATT_WINDOW, PAST_LEN)
    return {
        "x_prompt": nrm(ks[0], (BATCH, SEQ, D_MODEL), 1.0),
        "x_sample": nrm(ks[1], (DEC_BATCH, DEC_SEQ, D_MODEL), 1.0),
        "c_prompt": nrm(ks[2], (BATCH, D_MODEL), 1.0),
        "c_sample": nrm(ks[3], (DEC_BATCH, D_MODEL), 1.0),
        "cache_conv": nrm(ks[4], (DEPTH, DEC_BATCH, CONV_WIDTH - 1, D_CONV), 0.5),
        "cache_k": nrm(ks[5], (DEPTH, DEC_BATCH, W, N_HEADS, HEAD_DIM), 1.0),
        "cache_v": nrm(ks[6], (DEPTH, DEC_BATCH, W, N_HEADS, HEAD_DIM), 1.0),
        "g_pre": 1.0 + nrm(ks[7], (DEPTH, D_MODEL), 0.05),
        "g_post": 1.0 + nrm(ks[8], (DEPTH, D_MODEL), 0.05),
        "w_mod": nrm(ks[9], (DEPTH, D_MODEL, 3 * D_MODEL), 0.2 * D_MODEL ** -0.5),
        "b_mod": nrm(ks[10], (DEPTH, 3 * D_MODEL), 0.02),
        "w_in": nrm(ks[11], (DEPTH, D_MODEL, N_IN), D_MODEL ** -0.5),
        "dw_w": nrm(ks[12], (DEPTH, CONV_WIDTH, D_CONV), CONV_WIDTH ** -0.5),
        "dw_b": nrm(ks[13], (DEPTH, D_CONV), 0.02),
        "ln_g": 1.0 + nrm(ks[14], (DEPTH, D_CONV), 0.05),
        "ln_b": nrm(ks[15], (DEPTH, D_CONV), 0.02),
        "w_conv_out": nrm(ks[16], (DEPTH, D_CONV, D_MODEL), D_CONV ** -0.5),
        "rel_bias": nrm(ks[17], (DEPTH, N_HEADS, 2 * MAX_REL + 1), 0.5),
        "w_attn_out": nrm(ks[18], (DEPTH, D_ATTN, D_MODEL), D_ATTN ** -0.5),
        "w_o": nrm(ks[19], (DEPTH, D_MODEL, D_MODEL), D_MODEL ** -0.5),
    }


def reference(x_prompt, x_sample, c_prompt, c_sample, cache_conv, cache_k, cache_v,
              g_pre, g_post, w_mod, b_mod, w_in, dw_w, dw_b, ln_g, ln_b,
              w_conv_out, rel_bias, w_attn_out, w_o):
    xp, xs = x_prompt, x_sample
    conv_p, k_p, v_p, conv_s, k_s, v_s = [], [], [], [], [], []
    for l in range(DEPTH):
        u, zc, q, k, v, za, gc, ga, gate = layer_inputs(xp, c_prompt, g_pre[l], w_mod[l], b_mod[l], w_in[l])
        u_hist = jnp.concatenate([jnp.zeros((u.shape[0], CONV_WIDTH - 1, D_CONV), u.dtype), u], axis=1)
        y_conv = conv_branch(u_hist, zc, dw_w[l], dw_b[l], ln_g[l], ln_b[l], w_conv_out[l])
        o = prompt_band_attention(q, k, v, rel_bias[l])
        y_attn = attn_branch_out(o, za, w_attn_out[l])
        xp = layer_output(xp, y_conv, y_attn, gc, ga, gate, w_o[l], g_post[l])
        wp = min(ATT_WINDOW, k.shape[1])
        conv_p.append(u_hist[:, -(CONV_WIDTH - 1):])
        k_p.append(k[:, -wp:])
        v_p.append(v[:, -wp:])

        u, zc, q, k, v, za, gc, ga, gate = layer_inputs(xs, c_sample, g_pre[l], w_mod[l], b_mod[l], w_in[l])
        u_hist = jnp.concatenate([cache_conv[l], u], axis=1)
        y_conv = conv_branch(u_hist, zc, dw_w[l], dw_b[l], ln_g[l], ln_b[l], w_conv_out[l])
        k_all = jnp.concatenate([cache_k[l], k], axis=1)
        v_all = jnp.concatenate([cache_v[l], v], axis=1)
        o = sample_band_attention(q, k_all, v_all, rel_bias[l])
        y_attn = attn_branch_out(o, za, w_attn_out[l])
        xs = layer_output(xs, y_conv, y_attn, gc, ga, gate, w_o[l], g_post[l])
        ws = cache_k.shape[2]
        conv_s.append(u_hist[:, -(CONV_WIDTH - 1):])
        k_s.append(k_all[:, -ws:])
        v_s.append(v_all[:, -ws:])

    new_conv_prompt = jnp.stack(conv_p)
    new_k_prompt = jnp.stack(k_p)
    new_v_prompt = jnp.stack(v_p)
    new_conv_sample = jnp.stack(conv_s)
    new_k_sample = jnp.stack(k_s)
    new_v_sample = jnp.stack(v_s)
    return (xp, xs, new_conv_prompt, new_k_prompt, new_v_prompt, new_conv_sample, new_k_sample, new_v_sample)
```

```python
import os
from contextlib import ExitStack
import numpy as np
import concourse.bass as bass
import concourse.mybir as mybir
from concourse.bass_utils import run_bass_kernel_spmd

F32 = mybir.dt.float32
BF16 = mybir.dt.bfloat16
AF = mybir.ActivationFunctionType
ALU = mybir.AluOpType

D = 1024
NIN = 5632
T = 256
DEC_T = 32
PAST = 1024
EPS = 1e-6
ENGINES = ("pe", "act", "dve", "pool", "sp")
NEG = -32768.0


class Prog:
    def __init__(self, nc):
        self.nc = nc
        self.ops = {e: [] for e in ENGINES}
        self.nops = {e: 0 for e in ENGINES}
        self.last_w = {}
        self.readers = {}
        self.waited = {e: {} for e in ENGINES}
        self.dma_cnt = {}
        self.sem_names = set(ENGINES)
        self.phase = "setup"
        self.phases = {e: [] for e in ENGINES}

    def _deps(self, eng, reads, writes):
        need = {}

        def add(tok):
            if tok is None:
                return
            s, v, e = tok
            if need.get(s, 0) < v:
                need[s] = v
        for k in reads:
            add(self.last_w.get(k))
        for k in writes:
            tok = self.last_w.get(k)
            if tok is not None and (tok[2] != eng or eng != "pe"):
                add(tok)
            for r in self.readers.get(k, ()):
                if r[2] != eng or eng != "pe":
                    add(r)
        waits = []
        for s, v in need.items():
            if self.waited[eng].get(s, 0) >= v:
                continue
            self.waited[eng][s] = v
            waits.append((s, v))
        return waits

    def _commit(self, tok, reads, writes):
        for k in reads:
            if k not in writes:
                self.readers.setdefault(k, []).append(tok)
        for k in writes:
            self.last_w[k] = tok
            self.readers[k] = []

    def op(self, eng, emit, reads=(), writes=()):
        waits = self._deps(eng, reads, writes)
        self.nops[eng] += 1
        tok = (eng, self.nops[eng], eng)
        self.ops[eng].append((waits, emit, (eng, 1)))
        self.phases[eng].append(self.phase)
        self._commit(tok, reads, writes)

    def dma(self, queue, sem, emit, reads=(), writes=()):
        self.sem_names.add(sem)
        waits = self._deps(queue, reads, writes)
        self.dma_cnt[sem] = self.dma_cnt.get(sem, 0) + 16
        tok = (sem, self.dma_cnt[sem], None)
        self.ops[queue].append((waits, emit, (sem, 16)))
        self._commit(tok, reads, writes)

    def barrier(self, engines=ENGINES):
        allw = [(e, self.nops[e]) for e in ENGINES if self.nops[e] > 0]
        allw += [(s, v) for s, v in self.dma_cnt.items()]
        for e in engines:
            waits = []
            for s, v in allw:
                if self.waited[e].get(s, 0) >= v:
                    continue
                self.waited[e][s] = v
                waits.append((s, v))
            self.ops[e].append((waits, None, None))

    def emit_all(self, enter):
        nc = self.nc
        sems = {s: enter(nc.semaphore("sem_" + s)) for s in sorted(self.sem_names)}
        block = enter(nc.Block())
        handles = {"pe": "tensor", "act": "scalar", "dve": "vector", "pool": "gpsimd", "sp": "sync"}

        def make(engname):
            def body(eng):
                for waits, emit, inc in self.ops[engname]:
                    for s, v in waits:
                        eng.wait_ge(sems[s], v)
                    if emit is None:
                        continue
                    emit(eng).then_inc(sems[inc[0]], inc[1])
            return body
        for e in ENGINES:
            if self.ops[e]:
                getattr(block, handles[e])(make(e))


V_GPRE, V_BSH, V_BSC, V_DWB, V_LNG, V_LNB, V_DWW = 0, 8, 16, 24, 28, 32, 36
NV = 36 + 124


def build_program(nseq, seqlen, stop=99, tstop=99):
    NT = seqlen // T
    NS = nseq + 1
    WP = min(512, seqlen)
    nc = bass.Bass("TRN2", target_bir_lowering=False)

    def din(name, shape, dt=F32):
        return nc.dram_tensor(name, list(shape), dt, kind="ExternalInput").ap()

    def dout(name, shape, dt=F32):
        return nc.dram_tensor(name, list(shape), dt, kind="ExternalOutput").ap()

    xp = din("xp", [nseq, seqlen, D]); xs_d = din("xs", [DEC_T, D])
    cT_d = din("cT", [128, 8, NS])
    cconv_d = din("cconv", [30, 512]); ck_d = din("ck", [512, 512]); cv_d = din("cv", [512, 512])
    vecs_d = din("vecs", [128, NV]); gpost_d = din("gpost", [1, D]); bmod_d = din("bmod", [1, 3 * D])
    wmod_d = din("wmod", [D, 3 * D]); win_d = din("win", [D, NIN])
    wco_d = din("wco", [512, D]); wao_d = din("wao", [512, D]); wo_d = din("wo", [D, D])
    bt_d = din("bt", [128, 8, 256]); ch_d = din("ch", [128, 8]); id_d = din("ident", [128, 128])

    yp_d = dout("yp", [nseq, seqlen, D]); ys_d = dout("ys", [DEC_T, D])
    ncp_d = dout("ncp", [nseq, 30, 512]); nkp_d = dout("nkp", [nseq, WP, 512]); nvp_d = dout("nvp", [nseq, WP, 512])
    ncs_d = dout("ncs", [30, 512]); nks_d = dout("nks", [512, 512]); nvs_d = dout("nvs", [512, 512])
    wbf_d = nc.dram_tensor("wbf", [11, 128, 8 * 512], BF16, kind="Internal").ap()
    gate_d = nc.dram_tensor("gate_scr", [NS, D], F32, kind="Internal").ap()

    es = ExitStack()
    E = es.enter_context
    P = Prog(nc)

    def sb(name, shape, dt=F32):
        return E(nc.sbuf_tensor(name, list(shape), dt))

    xin = [sb(f"xin{i}", [128, 2, D]) for i in range(2)]
    sqj = sb("sqj", [128, D], BF16)
    xsb = [sb(f"xsb{i}", [128, D], BF16) for i in range(2)]
    hT = sb("hT", [128, 8, T], BF16)
    NW = 3
    wst = [sb(f"wst{i}", [128, 8, 512], BF16) for i in range(NW)]
    uring = sb("uring", [128, 4, 30 + T], BF16)
    sigb = sb("sigb", [128, 4, T])
    szc = sb("szc", [128, 4, T], BF16)
    qT = sb("qT", [128, 4, T], BF16)
    kring = sb("kring", [128, 4, 768], BF16)
    vaug = [sb(f"vaug{i}", [128, 8, 65], BF16) for i in range(6)]
    sza = sb("sza", [128, 4, T], BF16)
    sgc = sb("sgc", [128, 8, T], BF16)
    sga = sb("sga", [128, 8, T], BF16)
    PT = [[sb(f"PT{p}_{t}", [128, T], BF16) for t in range(6)] for p in range(2)]
    BThi = sb("BThi", [128, 8, 256], BF16)
    BTlo = sb("BTlo", [128, 8, 256], BF16)
    BT = wst[0].bitcast(F32)
    chc = sb("chc", [128, 8])
    ycv = sb("ycv", [128, 4, T])
    ybf = sb("ybf", [128, 4, T], BF16)
    ysq = sb("ysq", [128, 4, T], BF16)
    mean_sb = sb("mean_sb", [128, T]); m2 = sb("m2", [128, T]); lrstd = sb("lrstd", [128, T])
    tt4 = sb("tt4", [128, 4, T])
    t1 = [tt4[:, i, :] for i in range(2)]
    t2 = [tt4[:, 2 + i, :] for i in range(2)]
    cin = sb("cin", [128, 4, T], BF16)
    rc = sb("rc", [128, 8])
    onorm = [sb(f"onorm{i}", [128, 512], BF16) for i in range(2)]
    oT = sb("oT", [128, 4, T], BF16)
    merged = sb("merged", [128, 8, T], BF16)
    mt2 = [sb(f"mt{i}", [128, 2, T]) for i in range(2)]
    gg = sb("gg", [128, D])
    ytmp = [tt4[:, 2 * i:2 * i + 2, :].rearrange("p a t -> p (a t)") for i in range(2)]
    wco = sb("wco_sb", [128, 4, D], BF16); wao = sb("wao_sb", [128, 4, D], BF16); wo = sb("wo_sb", [128, 8, D], BF16)
    diag = sb("diag", [128, 4 * 31, 128], BF16)
    ident_f = sb("ident_f", [128, 128]); ident = sb("ident_b", [128, 128], BF16)
    ones_m = sb("ones_m", [128, 128], BF16)
    vecs = sb("vecs_sb", [128, NV])
    bmod3 = xin[0][0:NS, :, :].rearrange("p a d -> p (a d)")
    cT = sb("cT_sb", [128, 8, NS])
    modsb_g = tt4[0:NS, :, :].rearrange("p c t -> p (c t)")
    mod_sh = sigb[0:NS, :, :].rearrange("p c t -> p (c t)")
    mod_sc = ycv[0:NS, :, :].rearrange("p c t -> p (c t)")
    sel = sb("sel", [NS, NS, 128])
    gs = sb("gs", [128, NS, 8]); shf = sb("shf", [128, NS, 8])
    st8 = sb("st8", [128, 8])
    tht = sb("tht", [128, 2, T])
    mhalf = sb("mhalf", [128, 1])
    kstage = sb("kstage", [128, 512]); kstage_b = sb("kstage_b", [128, 512], BF16)
    ostage = [sb(f"ostage{i}", [128, 512]) for i in range(2)]
    wmst = [xin[1][:, i, :] for i in range(2)]

    def pt(name):
        return E(nc.psum_tensor(name, [128, 512], F32))
    ps_mm = [pt("ps_mm0"), pt("ps_mm1")]
    ps_cv = [pt("ps_cv0"), pt("ps_cv1")]
    ps_st = pt("ps_st"); ps_sc = pt("ps_sc")
    ps_pv = [pt("ps_pvA"), pt("ps_pvB")]
    ps_st_bf = ps_st.bitcast(BF16)
    ps_sc_bf = ps_sc.bitcast(BF16)
    ps_mm1_bf = ps_mm[1].bitcast(BF16)

    def ld(q, sem, out_ap, in_ap, wkeys, rkeys=()):
        P.dma(q, sem, lambda e: e.dma_start(out=out_ap, in_=in_ap), reads=rkeys, writes=wkeys)

    ld("sp", "ldc1", vecs[:], vecs_d[:, :], ["vecs"])
    ld("sp", "ldc2", ident_f[:], id_d[:, :], ["ident_f"])
    ld("sp", "ldc3", cT[:], cT_d[:, :, :], ["cT"])
    ld("sp", "ldc4", BT[:], bt_d[:, :, :], ["BT"])
    ld("sp", "ldc5", chc[:], ch_d[:, :], ["chc"])
    ld("sp", "ldc7", gg[0:NS, :], bmod_d[0:1, 2 * D:3 * D].partition_broadcast(NS), ["bgate3"])
    ld("sp", "ldc8", bmod3, bmod_d[0:1, 0:2 * D].partition_broadcast(NS), ["bmod3"])
    ld("pool", "ldw", wco[:], wco_d.rearrange("(k p) n -> p k n", p=128), ["wco"])
    ld("pool", "ldw", wao[:], wao_d.rearrange("(k p) n -> p k n", p=128), ["wao"])
    ld("pool", "ldw", wo[:], wo_d.rearrange("(k p) n -> p k n", p=128), ["wo"])
    for g in range(11):
        ld("pool", "ldw", wbf_d[g].rearrange("p (k n) -> p k n", k=8),
           win_d[:, g * 512:(g + 1) * 512].rearrange("(k p) n -> p k n", p=128), [("wbf", g)])

    P.op("dve", lambda e: e.tensor_copy(out=ident[:], in_=ident_f[:]), reads=["ident_f"], writes=["ident"])
    P.op("dve", lambda e: e.memset(ones_m[:], 1.0 / 512.0), writes=["ones_m"])
    P.op("dve", lambda e: e.memset(mhalf[:], -0.5), writes=["mhalf"])
    P.op("dve", lambda e: e.memset(sel[:], 0.0), writes=["sel"])
    for s in range(NS):
        pass
    for s in range(NS):
        P.op("dve", lambda e, s=s: e.tensor_copy(out=sel[:, s, :], in_=ident_f[0:NS, s:s + 1].broadcast_to([NS, 128])),
             reads=["ident_f", "sel"], writes=["sel"])
    for m in range(124):
        P.op("dve", lambda e, m=m: e.tensor_scalar(out=diag[:, m, :], in0=ident_f[:], scalar1=vecs[:, V_DWW + m:V_DWW + m + 1],
                                                   scalar2=None, op0=ALU.mult),
             reads=["vecs", "ident_f"], writes=[("diag", m)])
    for h in range(8):
        P.op("dve", lambda e, h=h: e.tensor_scalar(out=BT[:, h, :], in0=BT[:, h, :], scalar1=chc[:, h:h + 1], scalar2=None,
                                                   op0=ALU.subtract), reads=["BT", "chc"], writes=["BT"])
    P.op("dve", lambda e: e.memset(BT[64:128, :, 0:64], NEG), reads=["BT"], writes=["BT"])
    P.op("dve", lambda e: e.tensor_scalar(out=BT[:, :, :], in0=BT[:, :, :], scalar1=8.0, scalar2=None, op0=ALU.mult), reads=["BT"], writes=["BT"])
    P.op("dve", lambda e: e.tensor_copy(out=BThi[:], in_=BT[:, :, :]), reads=["BT"], writes=["BThi"])
    P.op("dve", lambda e: e.tensor_tensor(out=BTlo[:], in0=BT[:, :, :], in1=BThi[:], op=ALU.subtract), reads=["BT", "BThi"], writes=["BTlo"])
    for p_ in range(2):
        for t in range(6):
            P.op("pool", lambda e, p_=p_, t=t: e.memset(PT[p_][t][:], 0.0), writes=[("PT", p_, t)])
    for i in range(6):
        P.op("pool", lambda e, i=i: e.memset(vaug[i][:, :, 64:65], 1.0), writes=[("vaug1", i)])

    def finish():
        P.barrier()
        P.emit_all(E)
        es.close()
        return nc
    if stop == 1:
        return finish()
    banks6 = [ps_mm[0], ps_mm[1], ps_cv[0], ps_cv[1], ps_st, ps_sc]
    bkeys = [("mm", 0), ("mm", 1), ("cv", 0), ("cv", 1), ("st",), ("sc",)]
    bkeys_full = {0: [("mm", 0, 0), ("mm", 0, 1)], 1: [("mm", 1, 0), ("mm", 1, 1)], 2: [("cv", 0), ("cv", 1)],
                  3: [("cv", 2), ("cv", 3)], 4: [("st", 0), ("st", 1)], 5: [("sc", 0), ("sc", 1)]}
    for kc in range(8):
        for half in range(3):
            slot = (kc * 3 + half) % 2
            ld("sp", f"wm{slot}", wmst[slot][:], wmod_d[kc * 128:(kc + 1) * 128, half * 1024:(half + 1) * 1024], [("wmst", slot)])
            for j in range(2):
                cg = half * 2 + j
                P.op("pe", lambda e, kc=kc, cg=cg, j=j, slot=slot: e.matmul(
                    banks6[cg][0:NS, :], lhsT=cT[:, kc, :], rhs=wmst[slot][:, j * 512:(j + 1) * 512],
                    start=(kc == 0), stop=(kc == 7)), reads=["cT", ("wmst", slot)], writes=bkeys_full[cg])
    for cg in range(6):
        dst = (mod_sh, mod_sh, mod_sc, mod_sc, modsb_g, modsb_g)[cg]
        badd = (bmod3[:, cg * 512:(cg + 1) * 512] if cg < 4 else gg[0:NS, (cg - 4) * 512:(cg - 3) * 512])
        P.op("dve", lambda e, cg=cg, dst=dst, badd=badd: e.tensor_tensor(out=dst[:, (cg % 2) * 512:(cg % 2 + 1) * 512], in0=banks6[cg][0:NS, :],
                                                                       in1=badd, op=ALU.add),
             reads=bkeys_full[cg] + ["bmod3", "bgate3"], writes=[("modsb", cg)])
    for which in range(2):
        for fc in range(8):
            col = which * D + fc * 128
            msrc = (mod_sh, mod_sc)[which]
            P.op("pe", lambda e, which=which, fc=fc, msrc=msrc: e.transpose(
                out=ps_pv[0][:, (which * 8 + fc) * NS:(which * 8 + fc + 1) * NS], in_=msrc[:, fc * 128:(fc + 1) * 128],
                identity=ident_f[0:NS, 0:NS]), reads=[("modsb", col // 512), "ident_f"], writes=[("pv", 0)])
    for s in range(NS):
        P.op("dve", lambda e, s=s: e.tensor_copy(
            out=shf[:, s, :], in_=ps_pv[0][:, 0:8 * NS].rearrange("p (f s) -> p s f", s=NS)[:, s, :]),
            reads=[("pv", 0)], writes=["shf"])
        P.op("dve", lambda e, s=s: e.scalar_tensor_tensor(
            out=gs[:, s, :], in0=ps_pv[0][:, 8 * NS:16 * NS].rearrange("p (f s) -> p s f", s=NS)[:, s, :],
            scalar=1.0, in1=vecs[:, V_GPRE:V_GPRE + 8], op0=ALU.add, op1=ALU.mult),
            reads=[("pv", 0), "vecs"], writes=["gs"])
    ld("sp", "gst", gate_d[:, :], modsb_g, [], rkeys=[("modsb", 4), ("modsb", 5)])
    P.barrier()

    state = {"wslot": 0, "xslot": 0, "ost": 0, "tile": 0, "sbank": 0, "nb": 0, "tht": 0}
    dve_act = ["dve", "act"]

    def load_gg(s):
        tmpk = [("t1", 0), ("t1", 1), ("t2", 0), ("t2", 1)]
        tmp = tt4[:, :, :].rearrange("p c t -> p (c t)")
        ld("pool", "ggl", gg[:, :], gate_d[s:s + 1, :].partition_broadcast(128), [("gg", 0), ("gg", 1)])
        ld("pool", "ggl", tmp, gpost_d[0:1, :].partition_broadcast(128), tmpk)
        P.op("dve", lambda e: e.tensor_tensor(out=gg[:, :], in0=gg[:, :], in1=tmp, op=ALU.mult),
             reads=tmpk, writes=[("gg", 0), ("gg", 1)])

    def issue_x_load(x_ap, ntok, slot):
        if ntok == T:
            ld("pool", f"x{slot}", xin[slot][:, :, :], x_ap.rearrange("(t p) d -> p t d", p=128), [("xin", slot, 0), ("xin", slot, 1)])
        else:
            ld("pool", f"x{slot}", xin[slot][0:ntok, 0, :], x_ap, [("xin", slot, 0)])

    def tile(s, x_slot, ntok, c0, first, y_ap, kv_out=None, u_out=None, pre_mid=None):
        cur = ["prenorm"]
        TT = (ntok + 127) // 128
        rows = [min(128, ntok - tt * 128) for tt in range(TT)]
        nq = (ntok + 63) // 64
        cur[0] = "prenorm"; P.phase = "prenorm"
        for tt in range(TT):
            r = rows[tt]
            xt = xin[x_slot][0:r, tt, :]
            P.op("act", lambda e, r=r, xt=xt, tt=tt: e.activation(out=sqj[0:r, :], in_=xt, func=AF.Square, accum_out=st8[0:r, tt:tt + 1]),
                 reads=[("xin", x_slot, tt)], writes=["sqj", ("st8", tt)])
            P.op("dve", lambda e, r=r, tt=tt: e.tensor_scalar(out=st8[0:r, 2 + tt:3 + tt], in0=st8[0:r, tt:tt + 1], scalar1=1.0 / D, scalar2=EPS,
                                                              op0=ALU.mult, op1=ALU.add), reads=[("st8", tt)], writes=[("st8", 2 + tt)])
            P.op("pool", lambda e, r=r, tt=tt: e.tensor_tensor(out=st8[0:r, 6 + tt:7 + tt], in0=st8[0:r, 2 + tt:3 + tt], in1=mhalf[0:r, :], op=ALU.pow),
                 reads=[("st8", 2 + tt)], writes=[("st8", 6 + tt)])
            P.op("dve", lambda e, r=r, xt=xt, tt=tt: e.tensor_scalar(out=xsb[tt][0:r, :], in0=xt, scalar1=st8[0:r, 6 + tt:7 + tt], scalar2=None,
                                                                     op0=ALU.mult), reads=[("xin", x_slot, tt), ("st8", 6 + tt)], writes=[("xsb", tt)])
            xbf = (ps_st_bf, ps_mm1_bf)[tt % 2]
            xbk = [("st", 0), ("st", 1)] if tt % 2 == 0 else [("mm", 1, 0), ("mm", 1, 1)]
            pass
        yield "h0"
        P.phase = cur[0]
        for tt in range(TT):
            r = rows[tt]
            xbf = (ps_st_bf, ps_mm1_bf)[tt % 2]
            xbk = [("st", 0), ("st", 1)] if tt % 2 == 0 else [("mm", 1, 0), ("mm", 1, 1)]
            for fc in range(8):
                P.op("pe", lambda e, r=r, tt=tt, fc=fc, xbf=xbf: e.transpose(out=xbf[:, fc * 128:fc * 128 + r], in_=xsb[tt][0:r, fc * 128:(fc + 1) * 128],
                                                                             identity=ident[0:r, 0:r]),
                     reads=[("xsb", tt)], writes=xbk)
            xv = xbf[:, :].rearrange("p (f t) -> p f t", f=8)[:, :, 0:r]
            hv = hT[:, :, tt * 128:tt * 128 + r]
            hk = [("hT", fc, tt) for fc in range(8)]
            P.op("dve", lambda e, r=r, xv=xv, hv=hv: e.tensor_tensor(out=hv, in0=xv, in1=gs[:, s, :].unsqueeze(2).broadcast_to([128, 8, r]), op=ALU.mult),
                 reads=xbk, writes=hk)
            P.op("dve", lambda e, r=r, hv=hv: e.tensor_tensor(out=hv, in0=hv, in1=shf[:, s, :].unsqueeze(2).broadcast_to([128, 8, r]), op=ALU.add),
                 reads=hk, writes=hk)
        hT_keys = [("hT", fc, tt) for fc in range(8) for tt in range(TT)]

        yield "h1"

        P.phase = cur[0]
        cur[0] = "inproj"; P.phase = "inproj"
        mmslot = [0]

        def load_group(g):
            ws = state["wslot"]; state["wslot"] = (ws + 1) % NW
            ld("sp", f"w{ws}", wst[ws][:].rearrange("p k n -> p (k n)"), wbf_d[g], [("wst", ws)], rkeys=[("wbf", g)])
            return ws

        def fm_pair(ws, p, evac):
            sl = mmslot[0]; mmslot[0] = (sl + 1) % 2
            pk = [("mm", sl, 0), ("mm", sl, 1)]
            for half in range(2):
                c = 2 * p + half
                pa = ps_mm[sl][:, half * 256:half * 256 + ntok]
                for kc in range(8):
                    P.op("pe", lambda e, pa=pa, ws=ws, kc=kc, c=c: e.matmul(pa, lhsT=wst[ws][:, kc, c * 128:(c + 1) * 128], rhs=hT[:, kc, 0:ntok],
                                                                            start=(kc == 0), stop=(kc == 7)),
                         reads=[("wst", ws)] + hT_keys, writes=pk)
            bv = ps_mm[sl][:, :].rearrange("p (h t) -> p h t", h=2)[:, :, 0:ntok]
            evac(2 * p, bv, pk)

        def fm_group(g, evac):
            ws = load_group(g)
            for p in range(2):
                fm_pair(ws, p, evac)
            return ws

        def tm_group(g, evac, ws=None):
            if ws is None:
                ws = load_group(g)
            for tt in range(TT):
                r = rows[tt]
                bk = tt % 2
                pa = ps_mm[bk][0:r, :]
                pk = [("mm", bk, 0), ("mm", bk, 1)]
                for kc in range(8):
                    P.op("pe", lambda e, pa=pa, ws=ws, kc=kc, tt=tt, r=r: e.matmul(pa, lhsT=hT[:, kc, tt * 128:tt * 128 + r], rhs=wst[ws][:, kc, :],
                                                                                   start=(kc == 0), stop=(kc == 7)),
                         reads=[("wst", ws)] + hT_keys, writes=pk)
                evac(tt, r, pa, pk)
            mmslot[0] = 0
            return ws

        def out_stage(src_evac, dst_ap, r0, r1):
            so = state["ost"]; state["ost"] = (so + 1) % 2
            src_evac(ostage[so], ("ostage", so))
            ld("pool", f"os{so}", dst_ap, ostage[so][r0:r1, :], [], rkeys=[("ostage", so)])

        def silu2_fm(dst, key):
            def ev(c, pa, pk):
                P.op("act", lambda e: e.activation(out=tht[:, :, 0:ntok], in_=pa, func=AF.Tanh, scale=0.5), reads=pk, writes=["tht"])
                P.op("dve", lambda e: e.scalar_tensor_tensor(out=dst[:, c:c + 2, 0:ntok], in0=tht[:, :, 0:ntok], scalar=1.0, in1=pa, op0=ALU.add, op1=ALU.mult),
                     reads=pk + ["tht"], writes=[(key, c), (key, c + 1)])
            return ev

        def ev_b(c, pa, pk):
            P.op("act", lambda e: e.activation(out=sigb[:, c:c + 2, 0:ntok], in_=pa, func=AF.Tanh, scale=0.5), reads=pk, writes=[("sigb", c), ("sigb", c + 1)])
            P.op("dve", lambda e: e.tensor_scalar(out=sigb[:, c:c + 2, 0:ntok], in0=sigb[:, c:c + 2, 0:ntok], scalar1=0.5, scalar2=0.5, op0=ALU.mult, op1=ALU.add),
                 reads=[("sigb", c), ("sigb", c + 1)], writes=[("sigb", c), ("sigb", c + 1)])
        fm_group(1, ev_b)
        yield "h2"
        P.phase = cur[0]
        if first:
            P.op("dve", lambda e: e.memset(uring[:, :, 0:30], 0.0), writes=["uhist"])

        def ev_a(c, pa, pk):
            P.op("dve", lambda e: e.tensor_tensor(out=uring[:, c:c + 2, 30:30 + ntok], in0=pa, in1=sigb[:, c:c + 2, 0:ntok], op=ALU.mult),
                 reads=pk + [("sigb", c), ("sigb", c + 1)], writes=[("u", c), ("u", c + 1)])
        fm_group(0, ev_a)

        if u_out is not None:
            cur[0] = "uout"; P.phase = "uout"
            ttu = TT - 1; ru = rows[ttu]
            pk0 = [("mm", 0, 0), ("mm", 0, 1)]; pk1 = [("mm", 1, 0), ("mm", 1, 1)]
            pbu = ps_mm[0][0:ru, :]; pau = ps_mm[1][0:ru, :]
            wsb = load_group(1)
            for kc in range(8):
                P.op("pe", lambda e, kc=kc, pbu=pbu, ttu=ttu, ru=ru, wsb=wsb: e.matmul(
                    pbu, lhsT=hT[:, kc, ttu * 128:ttu * 128 + ru], rhs=wst[wsb][:, kc, :], start=(kc == 0), stop=(kc == 7)),
                    reads=[("wst", wsb)] + hT_keys, writes=pk0)
            P.op("act", lambda e, pbu=pbu, ru=ru: e.activation(out=kstage[0:ru, :], in_=pbu, func=AF.Tanh, scale=0.5), reads=pk0, writes=["kstage"])
            wsa = load_group(0)
            for kc in range(8):
                P.op("pe", lambda e, kc=kc, pau=pau, ttu=ttu, ru=ru, wsa=wsa: e.matmul(
                    pau, lhsT=hT[:, kc, ttu * 128:ttu * 128 + ru], rhs=wst[wsa][:, kc, :], start=(kc == 0), stop=(kc == 7)),
                    reads=[("wst", wsa)] + hT_keys, writes=pk1)

            def ev_u(o, ok, pau=pau, ru=ru):
                P.op("dve", lambda e: e.scalar_tensor_tensor(out=o[0:ru, :], in0=kstage[0:ru, :], scalar=1.0, in1=pau, op0=ALU.add, op1=ALU.mult),
                     reads=pk1 + ["kstage"], writes=[ok])
                P.op("dve", lambda e: e.tensor_scalar(out=o[0:ru, :], in0=o[0:ru, :], scalar1=0.5, scalar2=None, op0=ALU.mult), reads=[ok], writes=[ok])
            out_stage(ev_u, u_out, ru - 30, ru)
            mmslot[0] = 0
            cur[0] = "inproj"; P.phase = "inproj"

        def conv_chunk(c):
            ph = P.phase; P.phase = "conv"
            pc = (ps_sc, ps_st)[c % 2][:, 0:ntok]
            cvk = [(("sc", "st")[c % 2], 0), (("sc", "st")[c % 2], 1)]
            for k in range(31):
                P.op("pe", lambda e, pc=pc, c=c, k=k: e.matmul(pc, lhsT=diag[:, c * 31 + k, :], rhs=uring[:, c, k:k + ntok], start=(k == 0), stop=(k == 30)),
                     reads=[("u", c), "uhist"], writes=cvk)
            P.op("dve", lambda e, pc=pc, c=c: e.tensor_scalar(out=ycv[:, c, 0:ntok], in0=pc, scalar1=vecs[:, V_DWB + c:V_DWB + c + 1], scalar2=None, op0=ALU.add),
                 reads=cvk, writes=[("ycv", c)])
            P.op("act", lambda e, c=c: e.activation(out=ysq[:, c, 0:ntok], in_=ycv[:, c, 0:ntok], func=AF.Square),
                 reads=[("ycv", c)], writes=[("ysq", c)])
            P.op("dve", lambda e, c=c: e.tensor_copy(out=ybf[:, c, 0:ntok], in_=ycv[:, c, 0:ntok]), reads=[("ycv", c)], writes=[("ybf", c)])
            P.phase = ph

        yield "h3"

        P.phase = cur[0]
        fm_group(2, silu2_fm(szc, "szc"))
        yield "head_done"
        P.phase = cur[0]
        if pre_mid is not None:
            pre_mid()
        conv_chunk(0)

        def ev_q(c, pa, pk):
            P.op("dve", lambda e: e.tensor_copy(out=qT[:, c:c + 2, 0:ntok], in_=pa), reads=pk, writes=[("qT", c), ("qT", c + 1)])
        fm_group(3, ev_q)
        conv_chunk(1)

        kcol = (c0 % 12) * 64

        def ev_k(c, pa, pk):
            P.op("act", lambda e: e.activation(out=kring[:, c:c + 2, kcol:kcol + ntok], in_=pa, func=AF.Copy), reads=pk,
                 writes=[("k", cc, (c0 % 12) // 2 + j) for cc in (c, c + 1) for j in range((ntok + 127) // 128)])
        ws_k = fm_group(4, ev_k)
        if kv_out is not None:
            def ev_ktm(tt, r, pa, pk):
                dst = kv_out[0](tt, r)
                if dst is None:
                    return
                out_stage(lambda o, ok: P.op("act", lambda e: e.activation(out=o[0:r, :], in_=pa, func=AF.Copy), reads=pk, writes=[ok]), dst, 0, r)
            tm_group(4, ev_ktm, ws=ws_k)
        conv_chunk(2)

        a0 = c0 // 2

        def ev_v(tt, r, pa, pk):
            vs = (a0 + tt) % 6
            dst = kv_out[1](tt, r) if kv_out is not None else None
            if dst is None:
                P.op("dve", lambda e: e.tensor_copy(out=vaug[vs][0:r, :, 0:64], in_=pa.rearrange("p (h d) -> p h d", d=64)),
                     reads=pk, writes=[("v", vs)])
            else:
                so = state["ost"]; state["ost"] = (so + 1) % 2
                P.op("act", lambda e: e.activation(out=ostage[so][0:r, :], in_=pa, func=AF.Copy), reads=pk, writes=[("ostage", so)])
                P.op("dve", lambda e: e.tensor_copy(out=vaug[vs][0:r, :, 0:64], in_=ostage[so][0:r, :].rearrange("p (h d) -> p h d", d=64)),
                     reads=[("ostage", so)], writes=[("v", vs)])
                ld("pool", f"os{so}", dst, ostage[so][0:r, :], [], rkeys=[("ostage", so)])
        tm_group(5, ev_v)
        conv_chunk(3)

        cur[0] = "conv"; P.phase = "conv"
        P.op("dve", lambda e: e.tensor_copy(out=uring[:, :, 0:30], in_=uring[:, :, ntok:ntok + 30]),
             reads=[("u", c) for c in range(4)], writes=["uhist"])
        pmean = ps_st[:, 0:ntok]; pmsq = ps_st[:, 256:256 + ntok]
        for c in range(4):
            P.op("pe", lambda e, c=c: e.matmul(pmean, lhsT=ones_m[:], rhs=ybf[:, c, 0:ntok], start=(c == 0), stop=(c == 3)),
                 reads=[("ybf", c)], writes=[("st", 0), ("st", 1)])
        for c in range(4):
            P.op("pe", lambda e, c=c: e.matmul(pmsq, lhsT=ones_m[:], rhs=ysq[:, c, 0:ntok], start=(c == 0), stop=(c == 3)),
                 reads=[("ysq", c)], writes=[("st", 0), ("st", 1)])
        P.op("act", lambda e: e.activation(out=mean_sb[:, 0:ntok], in_=pmean, func=AF.Copy), reads=[("st", 0), ("st", 1)], writes=["mean_sb"])
        P.op("dve", lambda e: e.tensor_tensor(out=m2[:, 0:ntok], in0=mean_sb[:, 0:ntok], in1=mean_sb[:, 0:ntok], op=ALU.mult), reads=["mean_sb"], writes=["m2"])
        P.op("dve", lambda e: e.tensor_tensor(out=m2[:, 0:ntok], in0=pmsq, in1=m2[:, 0:ntok], op=ALU.subtract), reads=[("st", 0), ("st", 1), "m2"], writes=["m2"])
        P.op("dve", lambda e: e.tensor_scalar(out=m2[:, 0:ntok], in0=m2[:, 0:ntok], scalar1=0.0, scalar2=EPS, op0=ALU.max, op1=ALU.add), reads=["m2"], writes=["m2"])
        P.op("act", lambda e: e.activation(out=m2[:, 0:ntok], in_=m2[:, 0:ntok], func=AF.Sqrt), reads=["m2"], writes=["m2"])
        P.op("dve", lambda e: e.reciprocal(out=lrstd[:, 0:ntok], in_=m2[:, 0:ntok]), reads=["m2"], writes=["lrstd"])
        def ln_tail():
            cur[0] = "conv"; P.phase = "conv"
            allc = [("ycv", c) for c in range(4)]
            tk4 = [("t1", 0), ("t1", 1), ("t2", 0), ("t2", 1)]
            mean_b = mean_sb[:, 0:ntok].unsqueeze(1).broadcast_to([128, 4, ntok])
            rstd_b = lrstd[:, 0:ntok].unsqueeze(1).broadcast_to([128, 4, ntok])
            P.op("dve", lambda e: e.tensor_tensor(out=ycv[:, :, 0:ntok], in0=ycv[:, :, 0:ntok], in1=mean_b, op=ALU.subtract),
                 reads=allc + ["mean_sb"], writes=allc)
            P.op("dve", lambda e: e.tensor_tensor(out=ycv[:, :, 0:ntok], in0=ycv[:, :, 0:ntok], in1=rstd_b, op=ALU.mult),
                 reads=allc + ["lrstd"], writes=allc)
            for c in range(4):
                P.op("dve", lambda e, c=c: e.tensor_scalar(out=ycv[:, c, 0:ntok], in0=ycv[:, c, 0:ntok], scalar1=vecs[:, V_LNG + c:V_LNG + c + 1],
                                                           scalar2=vecs[:, V_LNB + c:V_LNB + c + 1], op0=ALU.mult, op1=ALU.add),
                     reads=[("ycv", c)], writes=[("ycv", c)])
            cur[0] = "attn"; P.phase = "attn"

        def ln_tail_b():
            cur[0] = "conv"; P.phase = "conv"
            allc = [("ycv", c) for c in range(4)]
            tk4 = [("t1", 0), ("t1", 1), ("t2", 0), ("t2", 1)]
            P.op("act", lambda e: e.activation(out=tt4[:, :, 0:ntok], in_=ycv[:, :, 0:ntok], func=AF.Tanh, scale=0.5), reads=allc, writes=tk4)
            P.op("dve", lambda e: e.scalar_tensor_tensor(out=tt4[:, :, 0:ntok], in0=tt4[:, :, 0:ntok], scalar=1.0, in1=ycv[:, :, 0:ntok], op0=ALU.add, op1=ALU.mult),
                 reads=allc + tk4, writes=tk4)
            P.op("dve", lambda e: e.scalar_tensor_tensor(out=cin[:, :, 0:ntok], in0=tt4[:, :, 0:ntok], scalar=0.25, in1=szc[:, :, 0:ntok], op0=ALU.mult, op1=ALU.mult),
                 reads=tk4 + [("szc", c) for c in range(4)], writes=[("cin", c) for c in range(4)])
            cur[0] = "attn"; P.phase = "attn"

        def ev_gate(dstg, keyg, off):
            def ev(c, pa, pk):
                P.op("act", lambda e: e.activation(out=dstg[:, off + c:off + c + 2, 0:ntok], in_=pa, func=AF.Tanh, scale=0.5), reads=pk,
                     writes=[(keyg, off + c), (keyg, off + c + 1)])
            return ev
        late = {0: [(6, silu2_fm(sza, "sza"))], 1: [(7, ev_gate(sgc, "sgc", 0))], 2: [(8, ev_gate(sgc, "sgc", 4))],
                3: [(9, ev_gate(sga, "sga", 0))], 4: [(10, ev_gate(sga, "sga", 4))]}

        cur[0] = "attn"; P.phase = "attn"
        kcur_end = c0 * 64 + ntok
        tiles = []
        for t in range(6):
            a = a0 - 4 + t
            if a < 0:
                continue
            nk = min(128, kcur_end - a * 128)
            if nk <= 0:
                continue
            tiles.append((t, a, nk))
        QB = (ntok + 127) // 128
        s_banks = [(ps_sc, "sc", 0), (ps_st, "st", 0), (ps_cv[0], "cv", 0), (ps_cv[1], "cv", 2)]
        for h in range(8):
            if h == 6:
                yield "mid_done"
                P.phase = cur[0]
            if h == 7:
                yield "t1"
                P.phase = cur[0]
            j, hp = h // 2, h % 2
            par = h % 2
            prow = slice(hp * 64, hp * 64 + 64)
            for (t, a, nk) in tiles:
                i_lo = max(0, 2 * t - 8); i_hi = min(nq - 1, 2 * t + 1)
                if i_lo > i_hi:
                    continue
                c_lo = i_lo * 64; c_hi = min((i_hi + 1) * 64, ntok)
                sbk = s_banks[state["sbank"] % 4]; state["sbank"] += 1
                psS = sbk[0][:, 0:256]
                sck = [(sbk[1], sbk[2]), (sbk[1], sbk[2] + 1)]
                kph = (a % 6) * 128
                near_t = max(0, 2 * t - 8) <= min(nq - 1, 2 * t - 5)
                P.op("pe", lambda e, psS=psS, nk=nk, c_lo=c_lo, c_hi=c_hi, j=j, prow=prow, kph=kph, near_t=near_t: e.matmul(
                    psS[0:nk, c_lo:c_hi], lhsT=kring[prow, j, kph:kph + nk], rhs=qT[prow, j, c_lo:c_hi], start=True, stop=(not near_t)),
                    reads=[("k", j, a % 6), ("qT", j)], writes=sck)
                ptile = PT[par][t]
                pkey = ("PT", par, t)
                n_lo = max(0, 2 * t - 8); n_hi = min(nq - 1, 2 * t - 5)
                has_near = n_lo <= n_hi
                if has_near:
                    q0 = n_lo * 64; q1 = min((n_hi + 1) * 64, ntok); w = q1 - q0
                    b0 = (8 + n_lo - 2 * t) * 64
                    for bi, btile in enumerate((BThi, BTlo)):
                        P.op("pe", lambda e, psS=psS, nk=nk, q0=q0, q1=q1, w=w, b0=b0, h=h, btile=btile, bi=bi: e.matmul(
                            psS[0:nk, q0:q1], lhsT=ident[0:nk, 0:nk], rhs=btile[0:nk, h, b0:b0 + w], start=False, stop=(bi == 1)),
                            reads=[], writes=sck)
                f_lo = max(0, 2 * t - 4); f_hi = min(nq - 1, 2 * t)
                has_far = f_lo <= f_hi
                if has_near or has_far:
                    e_lo = n_lo if has_near else f_lo
                    e_hi = f_hi if has_far else n_hi
                    q0 = e_lo * 64; q1 = min((e_hi + 1) * 64, ntok)
                    P.op("act", lambda e, ptile=ptile, psS=psS, nk=nk, q0=q0, q1=q1: e.activation(
                        out=ptile[0:nk, q0:q1], in_=psS[0:nk, q0:q1], func=AF.Exp, scale=0.125), reads=sck, writes=[pkey])
                ie = 2 * t + 1
                if ie <= nq - 1 and ie >= max(0, 2 * t - 4) and nk > 64:
                    q0 = ie * 64; q1 = min((ie + 1) * 64, ntok)
                    P.op("act", lambda e, ptile=ptile, psS=psS, nk=nk, q0=q0, q1=q1: e.activation(
                        out=ptile[64:nk, q0:q1], in_=psS[64:nk, q0:q1], func=AF.Exp, scale=0.125), reads=sck, writes=[pkey])
            for (g, ev) in late.get(h, ()):
                cur[0] = "inproj"; P.phase = "inproj"
                fm_group(g, ev)
                cur[0] = "attn"; P.phase = "attn"
            for qb in range(QB):
                r = rows[qb]
                bank = ps_pv[qb]
                hh = h % 4
                use = [(t, a, nk) for (t, a, nk) in tiles if qb <= t <= qb + 4]
                for idx, (t, a, nk) in enumerate(use):
                    P.op("pe", lambda e, bank=bank, hh=hh, r=r, qb=qb, t=t, a=a, nk=nk, idx=idx, nuse=len(use), par=par, h=h: e.matmul(
                        bank[0:r, hh * 65:hh * 65 + 65],
                        lhsT=PT[par][t][0:nk, qb * 128:qb * 128 + r], rhs=vaug[a % 6][0:nk, h, :], start=(idx == 0), stop=(idx == nuse - 1)),
                        reads=[("PT", par, t), ("v", a % 6), ("vaug1", a % 6)], writes=[("pv", qb)])
                P.op("dve", lambda e, bank=bank, hh=hh, r=r, h=h: e.reciprocal(out=rc[0:r, h:h + 1], in_=bank[0:r, hh * 65 + 64:hh * 65 + 65]),
                     reads=[("pv", qb)], writes=[("rc", h)])
                P.op("dve", lambda e, bank=bank, hh=hh, r=r, h=h, qb=qb: e.tensor_scalar(
                    out=onorm[qb][0:r, h * 64:(h + 1) * 64], in0=bank[0:r, hh * 65:hh * 65 + 64], scalar1=rc[0:r, h:h + 1], scalar2=None, op0=ALU.mult),
                    reads=[("pv", qb), ("rc", h)], writes=[("onorm", qb, h)])
            if h == 0:
                ln_tail()
            if h == 1:
                ln_tail_b()
        yield "t2"
        P.phase = cur[0]
        for qb in range(QB):
            r = rows[qb]
            obf = (ps_st_bf, ps_sc_bf)[qb % 2]
            obk = [(("st", "sc")[qb % 2], 0), (("st", "sc")[qb % 2], 1)]
            for fc in range(4):
                P.op("pe", lambda e, qb=qb, r=r, fc=fc, obf=obf: e.transpose(out=obf[:, fc * 128:fc * 128 + r], in_=onorm[qb][0:r, fc * 128:(fc + 1) * 128],
                                                                             identity=ident[0:r, 0:r]),
                     reads=[("onorm", qb, 2 * fc), ("onorm", qb, 2 * fc + 1)], writes=obk)
            for fc in range(4):
                P.op("dve", lambda e, qb=qb, r=r, fc=fc, obf=obf: e.scalar_tensor_tensor(out=oT[:, fc, qb * 128:qb * 128 + r], in0=obf[:, fc * 128:fc * 128 + r],
                                                                                         scalar=0.5, in1=sza[:, fc, qb * 128:qb * 128 + r], op0=ALU.mult, op1=ALU.mult),
                     reads=obk + [("sza", fc)], writes=[("oT", fc, qb)])

        yield "t3"

        P.phase = cur[0]
        cur[0] = "outproj"; P.phase = "outproj"
        for fp in range(4):
            fo0 = 2 * fp
            bka = (ps_mm[0], ps_mm[1], ps_cv[0], ps_cv[1])[fp]
            pak = ([("mm", 0, 0), ("mm", 0, 1)], [("mm", 1, 0), ("mm", 1, 1)], [("cv", 0), ("cv", 1)], [("cv", 2), ("cv", 3)])[fp]
            bkb = (ps_sc, ps_st, ps_pv[0], ps_pv[1])[fp]
            pbk = ([("sc", 0), ("sc", 1)], [("st", 0), ("st", 1)], [("pv", 0)], [("pv", 1)])[fp]
            for half in range(2):
                fo = fo0 + half
                pa = bka[:, half * 256:half * 256 + ntok]
                pb = bkb[:, half * 256:half * 256 + ntok]
                for kc in range(4):
                    P.op("pe", lambda e, pa=pa, kc=kc, fo=fo: e.matmul(pa, lhsT=wco[:, kc, fo * 128:(fo + 1) * 128], rhs=cin[:, kc, 0:ntok], start=(kc == 0), stop=(kc == 3)),
                         reads=[("cin", kc)], writes=pak)
                for kc in range(4):
                    P.op("pe", lambda e, pb=pb, kc=kc, fo=fo: e.matmul(pb, lhsT=wao[:, kc, fo * 128:(fo + 1) * 128], rhs=oT[:, kc, 0:ntok], start=(kc == 0), stop=(kc == 3)),
                         reads=[("oT", kc, qb) for qb in range(QB)], writes=pbk)
            va = bka[:, :].rearrange("p (h t) -> p h t", h=2)[:, :, 0:ntok]
            vb = bkb[:, :].rearrange("p (h t) -> p h t", h=2)[:, :, 0:ntok]
            P.op("dve", lambda e, va=va, fo0=fo0: e.scalar_tensor_tensor(out=mt2[0][:, :, 0:ntok], in0=sgc[:, fo0:fo0 + 2, 0:ntok], scalar=1.0, in1=va, op0=ALU.add, op1=ALU.mult),
                 reads=pak + [("sgc", fo0), ("sgc", fo0 + 1)], writes=[("mt", 0)])
            P.op("dve", lambda e, vb=vb, fo0=fo0: e.scalar_tensor_tensor(out=mt2[1][:, :, 0:ntok], in0=sga[:, fo0:fo0 + 2, 0:ntok], scalar=1.0, in1=vb, op0=ALU.add, op1=ALU.mult),
                 reads=pbk + [("sga", fo0), ("sga", fo0 + 1)], writes=[("mt", 1)])
            P.op("dve", lambda e, fo0=fo0: e.tensor_tensor(out=merged[:, fo0:fo0 + 2, 0:ntok], in0=mt2[0][:, :, 0:ntok], in1=mt2[1][:, :, 0:ntok], op=ALU.add),
                 reads=[("mt", 0), ("mt", 1)], writes=[("merged", fo0), ("merged", fo0 + 1)])

        yield "t4"

        P.phase = cur[0]
        cur[0] = "wo"; P.phase = "wo"
        def wok(tt, hf):
            return [("cv", 2 * hf), ("cv", 2 * hf + 1)] if tt % 2 == 0 else [("mm", hf, 0), ("mm", hf, 1)]
        for tt in range(TT):
            r = rows[tt]
            for hf in range(2):
                po = (ps_cv, ps_mm)[tt % 2][hf][0:r, :]
                for kc in range(8):
                    P.op("pe", lambda e, po=po, kc=kc, tt=tt, r=r, hf=hf: e.matmul(po, lhsT=merged[:, kc, tt * 128:tt * 128 + r], rhs=wo[:, kc, hf * 512:(hf + 1) * 512],
                                                                                   start=(kc == 0), stop=(kc == 7)),
                         reads=[("merged", kc)], writes=wok(tt, hf))
                P.op("act", lambda e, po=po, r=r, hf=hf: e.activation(out=sqj[0:r, 0:512], in_=po, func=AF.Square, accum_out=st8[0:r, hf:hf + 1]),
                     reads=wok(tt, hf), writes=["sqj", ("st8", hf)])
            P.op("dve", lambda e, r=r: e.tensor_tensor(out=st8[0:r, 2:3], in0=st8[0:r, 0:1], in1=st8[0:r, 1:2], op=ALU.add),
                 reads=[("st8", 0), ("st8", 1)], writes=[("st8", 2)])
            P.op("dve", lambda e, r=r: e.tensor_scalar(out=st8[0:r, 3:4], in0=st8[0:r, 2:3], scalar1=1.0 / D, scalar2=4.0 * EPS, op0=ALU.mult, op1=ALU.add),
                 reads=[("st8", 2)], writes=[("st8", 3)])
            P.op("pool", lambda e, r=r: e.tensor_tensor(out=st8[0:r, 6:7], in0=st8[0:r, 3:4], in1=mhalf[0:r, :], op=ALU.pow), reads=[("st8", 3)], writes=[("st8", 6)])
            for hf in range(2):
                po = (ps_cv, ps_mm)[tt % 2][hf][0:r, :]
                P.op("dve", lambda e, po=po, r=r, hf=hf: e.scalar_tensor_tensor(out=ytmp[hf][0:r, :], in0=po, scalar=st8[0:r, 6:7], in1=gg[0:r, hf * 512:(hf + 1) * 512],
                                                                                op0=ALU.mult, op1=ALU.mult),
                     reads=wok(tt, hf) + [("st8", 6), ("gg", hf)], writes=[(("t1", "t2")[hf], 0), (("t1", "t2")[hf], 1)])
                P.op("dve", lambda e, r=r, hf=hf, tt=tt: e.tensor_tensor(out=xin[x_slot][0:r, tt, hf * 512:(hf + 1) * 512], in0=xin[x_slot][0:r, tt, hf * 512:(hf + 1) * 512],
                                                                          in1=ytmp[hf][0:r, :], op=ALU.add),
                     reads=[(("t1", "t2")[hf], 0), (("t1", "t2")[hf], 1), ("xin", x_slot, tt)], writes=[("xin", x_slot, tt)])
        if ntok == T:
            ld("pool", f"yo{x_slot}", y_ap.rearrange("(t p) d -> p t d", p=128), xin[x_slot][:, :, :], [], rkeys=[("xin", x_slot, 0), ("xin", x_slot, 1)])
        else:
            ld("pool", f"yo{x_slot}", y_ap, xin[x_slot][0:ntok, 0, :], [], rkeys=[("xin", x_slot, 0)])

    s_samp = nseq
    issue_x_load(xs_d[:, :], DEC_T, 0)
    for i in range(4):
        a = 4 + i
        ld("sp", "ldk", kstage[:], ck_d[i * 128:(i + 1) * 128, :], ["kstage"])
        P.op("dve", lambda e: e.tensor_copy(out=kstage_b[:], in_=kstage[:]), reads=["kstage"], writes=["kstage_b"])
        for j in range(4):
            P.op("pe", lambda e, j=j: e.transpose(out=ps_st_bf[:, j * 128:(j + 1) * 128], in_=kstage_b[:, j * 128:(j + 1) * 128], identity=ident[:]),
                 reads=["kstage_b"], writes=[("st", 0), ("st", 1)])
        P.op("act", lambda e, a=a: e.activation(out=kring[:, :, (a % 6) * 128:(a % 6) * 128 + 128],
                                                in_=ps_st_bf[:, 0:512].rearrange("p (j t) -> p j t", j=4), func=AF.Copy),
             reads=[("st", 0), ("st", 1)], writes=[("k", j, a % 6) for j in range(4)])
        ld("sp", f"ldv{i}", ostage[i % 2][:], cv_d[i * 128:(i + 1) * 128, :], [("ostage", i % 2)])
        P.op("dve", lambda e, a=a, i=i: e.tensor_copy(out=vaug[a % 6][:, :, 0:64], in_=ostage[i % 2][:].rearrange("p (h d) -> p h d", d=64)),
             reads=[("ostage", i % 2)], writes=[("v", a % 6)])
    ld("sp", "ldk", kstage[0:30, :], cconv_d[:, :], ["kstage"])
    P.op("dve", lambda e: e.tensor_copy(out=kstage_b[0:30, :], in_=kstage[0:30, :]), reads=["kstage"], writes=["kstage_b"])
    for c in range(4):
        P.op("pe", lambda e, c=c: e.transpose(out=ps_st_bf[:, c * 128:c * 128 + 30], in_=kstage_b[0:30, c * 128:(c + 1) * 128], identity=ident[0:30, 0:30]),
             reads=["kstage_b"], writes=[("st", 0), ("st", 1)])
    P.op("act", lambda e: e.activation(out=uring[:, :, 0:30], in_=ps_st_bf[:, 0:512].rearrange("p (c t) -> p c t", c=4)[:, :, 0:30], func=AF.Copy),
         reads=[("st", 0), ("st", 1)], writes=["uhist"])
    ld("sp", "cpy", nks_d[0:480, :], ck_d[32:512, :], [])
    ld("sp", "cpy", nvs_d[0:480, :], cv_d[32:512, :], [])
    descs = [dict(s=s_samp, ntok=DEC_T, c0=PAST // 64, first=False, y=ys_d[:, :], x=xs_d[:, :],
                  kv=(lambda tt, r: nks_d[480:512, :], lambda tt, r: nvs_d[480:512, :]), u=ncs_d[:, :],
                  pre_mid=(lambda: load_gg(s_samp)))]
    for b in range(nseq):
        for n in range(NT):
            tok0 = n * T
            kvo = None
            if tok0 + T > seqlen - WP:
                def kdst(tt, r, b=b, tok0=tok0):
                    p0 = tok0 + tt * 128 - (seqlen - WP)
                    return nkp_d[b, p0:p0 + r, :] if p0 >= 0 else None

                def vdst(tt, r, b=b, tok0=tok0):
                    p0 = tok0 + tt * 128 - (seqlen - WP)
                    return nvp_d[b, p0:p0 + r, :] if p0 >= 0 else None
                kvo = (kdst, vdst)
            descs.append(dict(s=b, ntok=T, c0=n * 4, first=(n == 0), y=yp_d[b, tok0:tok0 + T, :], x=xp[b, tok0:tok0 + T, :],
                              kv=kvo, u=(ncp_d[b, :, :] if n == NT - 1 else None),
                              pre_mid=((lambda b=b: load_gg(b)) if n == 0 else None)))
    gens = [tile(d["s"], k % 2, d["ntok"], d["c0"], d["first"], d["y"], kv_out=d["kv"], u_out=d["u"], pre_mid=d["pre_mid"])
            for k, d in enumerate(descs)]

    def run_until(g, label):
        while next(g) != label:
            pass

    issue_x_load(descs[0]["x"], descs[0]["ntok"], 0)
    if len(descs) > 1:
        issue_x_load(descs[1]["x"], descs[1]["ntok"], 1)
    run_until(gens[0], "head_done")
    for k in range(len(descs)):
        nxt = gens[k + 1] if k + 1 < len(descs) else None
        if nxt is not None:
            run_until(nxt, "h0")
        run_until(gens[k], "mid_done")
        for tail_lbl, head_lbl in (("t1", "h1"), ("t2", "h2"), ("t3", "h3"), ("t4", "head_done")):
            run_until(gens[k], tail_lbl)
            if nxt is not None:
                run_until(nxt, head_lbl)
        for _ in gens[k]:
            pass
        if k + 2 < len(descs):
            issue_x_load(descs[k + 2]["x"], descs[k + 2]["ntok"], k % 2)

    P.barrier()
    if os.environ.get("PHASE_DUMP"):
        import json
        json.dump(P.phases, open(os.environ["PHASE_DUMP"], "w"))
    P.emit_all(E)
    es.close()
    return nc


def _bias_gather_index():
    k = np.arange(128)[:, None]
    col = np.arange(256)[None, :]
    return np.clip(col - k, -128, 128) + 128


def make_core_inputs(inp, core, nseq, seqlen, shared):
    c_rows = [inp["c_prompt"][core * nseq + b] for b in range(nseq)] + [inp["c_sample"][core]]
    cmat = np.stack(c_rows, axis=0)
    cT = np.ascontiguousarray(cmat.T.reshape(8, 128, -1).transpose(1, 0, 2))
    m = {
        "xp": np.ascontiguousarray(inp["x_prompt"][core * nseq:(core + 1) * nseq]),
        "xs": np.ascontiguousarray(inp["x_sample"][core]),
        "cT": cT,
        "cconv": np.ascontiguousarray(inp["cache_conv"][0, core]),
        "ck": np.ascontiguousarray(inp["cache_k"][0, core].reshape(512, 512)),
        "cv": np.ascontiguousarray(inp["cache_v"][0, core].reshape(512, 512)),
    }
    m.update(shared)
    return m


def make_shared(inp):
    def cols(v, n):
        return np.asarray(v, np.float32).reshape(n, 128).T
    b_mod = np.asarray(inp["b_mod"][0], np.float32)
    dw_w = np.asarray(inp["dw_w"][0], np.float32)
    dww = dw_w.reshape(31, 4, 128).transpose(2, 1, 0).reshape(128, 124)
    vecs = np.concatenate([cols(inp["g_pre"][0], 8), cols(b_mod[0:D], 8), cols(b_mod[D:2 * D], 8), cols(inp["dw_b"][0], 4),
                           cols(inp["ln_g"][0], 4), cols(inp["ln_b"][0], 4), dww], axis=1).astype(np.float32)
    rb = np.asarray(inp["rel_bias"][0], np.float32)
    bt = np.ascontiguousarray(rb[:, _bias_gather_index()].transpose(1, 0, 2))
    ch = np.ascontiguousarray(np.broadcast_to(rb[:, 256][None, :], (128, 8)))
    return {
        "vecs": np.ascontiguousarray(vecs), "gpost": np.asarray(inp["g_post"], np.float32).reshape(1, D),
        "bmod": b_mod.reshape(1, 3 * D), "wmod": np.ascontiguousarray(inp["w_mod"][0]), "win": np.ascontiguousarray(inp["w_in"][0]),
        "wco": np.ascontiguousarray(inp["w_conv_out"][0]), "wao": np.ascontiguousarray(inp["w_attn_out"][0]),
        "wo": np.ascontiguousarray(inp["w_o"][0]), "bt": bt, "ch": ch, "ident": np.eye(128, dtype=np.float32),
    }


def run(inp, ncores, nseq, seqlen, stop=99, tstop=99):
    inp = {k: np.asarray(v) for k, v in inp.items()}
    nc = build_program(nseq, seqlen, stop, tstop)
    shared = make_shared(inp)
    in_maps = [make_core_inputs(inp, c, nseq, seqlen, shared) for c in range(ncores)]
    res = run_bass_kernel_spmd(nc, in_maps, core_ids=list(range(ncores)))
    R = res.results
    WP = min(512, seqlen)
    cat = lambda k: np.concatenate([r[k] for r in R], axis=0)
    stk = lambda k: np.stack([r[k] for r in R], axis=0)
    yp = cat("yp"); ys = stk("ys")
    ncp = cat("ncp")[None]; nkp = cat("nkp").reshape(1, ncores * nseq, WP, 8, 64); nvp = cat("nvp").reshape(1, ncores * nseq, WP, 8, 64)
    ncs = stk("ncs")[None]; nks = stk("nks").reshape(1, ncores, 512, 8, 64); nvs = stk("nvs").reshape(1, ncores, 512, 8, 64)
    return tuple(np.ascontiguousarray(a, dtype=np.float32) for a in (yp, ys, ncp, nkp, nvp, ncs, nks, nvs))


def kernel(**inputs):
    return run(inputs, 8, 2, 4096)
```

```python
import os
from contextlib import ExitStack
import numpy as np
import concourse.bass as bass
import concourse.mybir as mybir
from concourse.bass_utils import run_bass_kernel_spmd

F32 = mybir.dt.float32
BF16 = mybir.dt.bfloat16
AF = mybir.ActivationFunctionType
ALU = mybir.AluOpType

D = 1024
NIN = 5632
T = 256
DEC_T = 32
PAST = 1024
EPS = 1e-6
ENGINES = ("pe", "act", "dve", "pool", "sp")
NEG = -32768.0


class Prog:
    def __init__(self, nc):
        self.nc = nc
        self.ops = {e: [] for e in ENGINES}
        self.nops = {e: 0 for e in ENGINES}
        self.last_w = {}
        self.readers = {}
        self.waited = {e: {} for e in ENGINES}
        self.dma_cnt = {}
        self.sem_names = set(ENGINES)
        self.phase = "setup"
        self.phases = {e: [] for e in ENGINES}

    def _deps(self, eng, reads, writes):
        need = {}

        def add(tok):
            if tok is None:
                return
            s, v, e = tok
            if need.get(s, 0) < v:
                need[s] = v
        for k in reads:
            add(self.last_w.get(k))
        for k in writes:
            tok = self.last_w.get(k)
            if tok is not None and (tok[2] != eng or eng != "pe"):
                add(tok)
            for r in self.readers.get(k, ()):
                if r[2] != eng or eng != "pe":
                    add(r)
        waits = []
        for s, v in need.items():
            if self.waited[eng].get(s, 0) >= v:
                continue
            self.waited[eng][s] = v
            waits.append((s, v))
        return waits

    def _commit(self, tok, reads, writes):
        for k in reads:
            if k not in writes:
                self.readers.setdefault(k, []).append(tok)
        for k in writes:
            self.last_w[k] = tok
            self.readers[k] = []

    def op(self, eng, emit, reads=(), writes=()):
        waits = self._deps(eng, reads, writes)
        self.nops[eng] += 1
        tok = (eng, self.nops[eng], eng)
        self.ops[eng].append((waits, emit, (eng, 1)))
        self.phases[eng].append(self.phase)
        self._commit(tok, reads, writes)

    def dma(self, queue, sem, emit, reads=(), writes=()):
        self.sem_names.add(sem)
        waits = self._deps(queue, reads, writes)
        self.dma_cnt[sem] = self.dma_cnt.get(sem, 0) + 16
        tok = (sem, self.dma_cnt[sem], None)
        self.ops[queue].append((waits, emit, (sem, 16)))
        self._commit(tok, reads, writes)

    def barrier(self, engines=ENGINES):
        allw = [(e, self.nops[e]) for e in ENGINES if self.nops[e] > 0]
        allw += [(s, v) for s, v in self.dma_cnt.items()]
        for e in engines:
            waits = []
            for s, v in allw:
                if self.waited[e].get(s, 0) >= v:
                    continue
                self.waited[e][s] = v
                waits.append((s, v))
            self.ops[e].append((waits, None, None))

    def emit_all(self, enter):
        nc = self.nc
        sems = {s: enter(nc.semaphore("sem_" + s)) for s in sorted(self.sem_names)}
        block = enter(nc.Block())
        handles = {"pe": "tensor", "act": "scalar", "dve": "vector", "pool": "gpsimd", "sp": "sync"}

        def make(engname):
            def body(eng):
                for waits, emit, inc in self.ops[engname]:
                    for s, v in waits:
                        eng.wait_ge(sems[s], v)
                    if emit is None:
                        continue
                    emit(eng).then_inc(sems[inc[0]], inc[1])
            return body
        for e in ENGINES:
            if self.ops[e]:
                getattr(block, handles[e])(make(e))


V_GPRE, V_BSH, V_BSC, V_DWB, V_LNG, V_LNB, V_DWW = 0, 8, 16, 24, 28, 32, 36
NV = 36 + 124


def build_program(nseq, seqlen, stop=99, tstop=99):
    NT = seqlen // T
    NS = nseq + 1
    WP = min(512, seqlen)
    nc = bass.Bass("TRN2", target_bir_lowering=False)

    def din(name, shape, dt=F32):
        return nc.dram_tensor(name, list(shape), dt, kind="ExternalInput").ap()

    def dout(name, shape, dt=F32):
        return nc.dram_tensor(name, list(shape), dt, kind="ExternalOutput").ap()

    xp = din("xp", [nseq, seqlen, D]); xs_d = din("xs", [DEC_T, D])
    cT_d = din("cT", [128, 8, NS])
    cconv_d = din("cconv", [30, 512]); ck_d = din("ck", [512, 512]); cv_d = din("cv", [512, 512])
    vecs_d = din("vecs", [128, NV]); gpost_d = din("gpost", [1, D]); bmod_d = din("bmod", [1, 3 * D])
    wmod_d = din("wmod", [D, 3 * D]); win_d = din("win", [D, NIN])
    wco_d = din("wco", [512, D]); wao_d = din("wao", [512, D]); wo_d = din("wo", [D, D])
    bt_d = din("bt", [128, 8, 256]); ch_d = din("ch", [128, 8]); id_d = din("ident", [128, 128])

    yp_d = dout("yp", [nseq, seqlen, D]); ys_d = dout("ys", [DEC_T, D])
    ncp_d = dout("ncp", [nseq, 30, 512]); nkp_d = dout("nkp", [nseq, WP, 512]); nvp_d = dout("nvp", [nseq, WP, 512])
    ncs_d = dout("ncs", [30, 512]); nks_d = dout("nks", [512, 512]); nvs_d = dout("nvs", [512, 512])
    wbf_d = nc.dram_tensor("wbf", [11, 128, 8 * 512], BF16, kind="Internal").ap()
    gate_d = nc.dram_tensor("gate_scr", [NS, D], F32, kind="Internal").ap()

    es = ExitStack()
    E = es.enter_context
    P = Prog(nc)

    def sb(name, shape, dt=F32):
        return E(nc.sbuf_tensor(name, list(shape), dt))

    xin = [sb(f"xin{i}", [128, 2, D]) for i in range(2)]
    sqj = sb("sqj", [128, D], BF16)
    xsb = [sb(f"xsb{i}", [128, D], BF16) for i in range(2)]
    hT = sb("hT", [128, 8, T], BF16)
    NW = 3
    wst = [sb(f"wst{i}", [128, 8, 512], BF16) for i in range(NW)]
    uring = sb("uring", [128, 4, 30 + T], BF16)
    sigb = sb("sigb", [128, 4, T])
    szc = sb("szc", [128, 4, T], BF16)
    qT = sb("qT", [128, 4, T], BF16)
    kring = sb("kring", [128, 4, 768], BF16)
    vaug = [sb(f"vaug{i}", [128, 8, 65], BF16) for i in range(6)]
    sza = sb("sza", [128, 4, T], BF16)
    sgc = sb("sgc", [128, 8, T], BF16)
    sga = sb("sga", [128, 8, T], BF16)
    PT = [[sb(f"PT{p}_{t}", [128, T], BF16) for t in range(6)] for p in range(2)]
    BThi = sb("BThi", [128, 8, 256], BF16)
    BTlo = sb("BTlo", [128, 8, 256], BF16)
    BT = wst[0].bitcast(F32)
    chc = sb("chc", [128, 8])
    ycv = sb("ycv", [128, 4, T])
    ybf = sb("ybf", [128, 4, T], BF16)
    ysq = sb("ysq", [128, 4, T], BF16)
    mean_sb = sb("mean_sb", [128, T]); m2 = sb("m2", [128, T]); lrstd = sb("lrstd", [128, T])
    tt4 = sb("tt4", [128, 4, T])
    t1 = [tt4[:, i, :] for i in range(2)]
    t2 = [tt4[:, 2 + i, :] for i in range(2)]
    cin = sb("cin", [128, 4, T], BF16)
    rc = sb("rc", [128, 8])
    onorm = [sb(f"onorm{i}", [128, 512], BF16) for i in range(2)]
    oT = sb("oT", [128, 4, T], BF16)
    merged = sb("merged", [128, 8, T], BF16)
    mt2 = [sb(f"mt{i}", [128, 2, T]) for i in range(2)]
    gg = sb("gg", [128, D])
    ytmp = [tt4[:, 2 * i:2 * i + 2, :].rearrange("p a t -> p (a t)") for i in range(2)]
    wco = sb("wco_sb", [128, 4, D], BF16); wao = sb("wao_sb", [128, 4, D], BF16); wo = sb("wo_sb", [128, 8, D], BF16)
    diag = sb("diag", [128, 4 * 31, 128], BF16)
    ident_f = sb("ident_f", [128, 128]); ident = sb("ident_b", [128, 128], BF16)
    ones_m = sb("ones_m", [128, 128], BF16)
    vecs = sb("vecs_sb", [128, NV])
    bmod3 = xin[0][0:NS, :, :].rearrange("p a d -> p (a d)")
    cT = sb("cT_sb", [128, 8, NS])
    modsb_g = tt4[0:NS, :, :].rearrange("p c t -> p (c t)")
    mod_sh = sigb[0:NS, :, :].rearrange("p c t -> p (c t)")
    mod_sc = ycv[0:NS, :, :].rearrange("p c t -> p (c t)")
    sel = sb("sel", [NS, NS, 128])
    gs = sb("gs", [128, NS, 8]); shf = sb("shf", [128, NS, 8])
    st8 = sb("st8", [128, 8])
    tht = sb("tht", [128, 2, T])
    mhalf = sb("mhalf", [128, 1])
    kstage = sb("kstage", [128, 512]); kstage_b = sb("kstage_b", [128, 512], BF16)
    ostage = [sb(f"ostage{i}", [128, 512]) for i in range(2)]
    wmst = [xin[1][:, i, :] for i in range(2)]

    def pt(name):
        return E(nc.psum_tensor(name, [128, 512], F32))
    ps_mm = [pt("ps_mm0"), pt("ps_mm1")]
    ps_cv = [pt("ps_cv0"), pt("ps_cv1")]
    ps_st = pt("ps_st"); ps_sc = pt("ps_sc")
    ps_pv = [pt("ps_pvA"), pt("ps_pvB")]
    ps_st_bf = ps_st.bitcast(BF16)
    ps_sc_bf = ps_sc.bitcast(BF16)
    ps_mm1_bf = ps_mm[1].bitcast(BF16)

    def ld(q, sem, out_ap, in_ap, wkeys, rkeys=()):
        P.dma(q, sem, lambda e: e.dma_start(out=out_ap, in_=in_ap), reads=rkeys, writes=wkeys)

    ld("sp", "ldc1", vecs[:], vecs_d[:, :], ["vecs"])
    ld("sp", "ldc2", ident_f[:], id_d[:, :], ["ident_f"])
    ld("sp", "ldc3", cT[:], cT_d[:, :, :], ["cT"])
    ld("sp", "ldc4", BT[:], bt_d[:, :, :], ["BT"])
    ld("sp", "ldc5", chc[:], ch_d[:, :], ["chc"])
    ld("sp", "ldc7", gg[0:NS, :], bmod_d[0:1, 2 * D:3 * D].partition_broadcast(NS), ["bgate3"])
    ld("sp", "ldc8", bmod3, bmod_d[0:1, 0:2 * D].partition_broadcast(NS), ["bmod3"])
    ld("pool", "ldw", wco[:], wco_d.rearrange("(k p) n -> p k n", p=128), ["wco"])
    ld("pool", "ldw", wao[:], wao_d.rearrange("(k p) n -> p k n", p=128), ["wao"])
    ld("pool", "ldw", wo[:], wo_d.rearrange("(k p) n -> p k n", p=128), ["wo"])
    for g in range(11):
        ld("pool", "ldw", wbf_d[g].rearrange("p (k n) -> p k n", k=8),
           win_d[:, g * 512:(g + 1) * 512].rearrange("(k p) n -> p k n", p=128), [("wbf", g)])

    P.op("dve", lambda e: e.tensor_copy(out=ident[:], in_=ident_f[:]), reads=["ident_f"], writes=["ident"])
    P.op("dve", lambda e: e.memset(ones_m[:], 1.0 / 512.0), writes=["ones_m"])
    P.op("dve", lambda e: e.memset(mhalf[:], -0.5), writes=["mhalf"])
    P.op("dve", lambda e: e.memset(sel[:], 0.0), writes=["sel"])
    for s in range(NS):
        pass
    for s in range(NS):
        P.op("dve", lambda e, s=s: e.tensor_copy(out=sel[:, s, :], in_=ident_f[0:NS, s:s + 1].broadcast_to([NS, 128])),
             reads=["ident_f", "sel"], writes=["sel"])
    for m in range(124):
        P.op("dve", lambda e, m=m: e.tensor_scalar(out=diag[:, m, :], in0=ident_f[:], scalar1=vecs[:, V_DWW + m:V_DWW + m + 1],
                                                   scalar2=None, op0=ALU.mult),
             reads=["vecs", "ident_f"], writes=[("diag", m)])
    for h in range(8):
        P.op("dve", lambda e, h=h: e.tensor_scalar(out=BT[:, h, :], in0=BT[:, h, :], scalar1=chc[:, h:h + 1], scalar2=None,
                                                   op0=ALU.subtract), reads=["BT", "chc"], writes=["BT"])
    P.op("dve", lambda e: e.memset(BT[64:128, :, 0:64], NEG), reads=["BT"], writes=["BT"])
    P.op("dve", lambda e: e.tensor_scalar(out=BT[:, :, :], in0=BT[:, :, :], scalar1=8.0, scalar2=None, op0=ALU.mult), reads=["BT"], writes=["BT"])
    P.op("dve", lambda e: e.tensor_copy(out=BThi[:], in_=BT[:, :, :]), reads=["BT"], writes=["BThi"])
    P.op("dve", lambda e: e.tensor_tensor(out=BTlo[:], in0=BT[:, :, :], in1=BThi[:], op=ALU.subtract), reads=["BT", "BThi"], writes=["BTlo"])
    for p_ in range(2):
        for t in range(6):
            P.op("pool", lambda e, p_=p_, t=t: e.memset(PT[p_][t][:], 0.0), writes=[("PT", p_, t)])
    for i in range(6):
        P.op("pool", lambda e, i=i: e.memset(vaug[i][:, :, 64:65], 1.0), writes=[("vaug1", i)])

    def finish():
        P.barrier()
        P.emit_all(E)
        es.close()
        return nc
    if stop == 1:
        return finish()
    banks6 = [ps_mm[0], ps_mm[1], ps_cv[0], ps_cv[1], ps_st, ps_sc]
    bkeys = [("mm", 0), ("mm", 1), ("cv", 0), ("cv", 1), ("st",), ("sc",)]
    bkeys_full = {0: [("mm", 0, 0), ("mm", 0, 1)], 1: [("mm", 1, 0), ("mm", 1, 1)], 2: [("cv", 0), ("cv", 1)],
                  3: [("cv", 2), ("cv", 3)], 4: [("st", 0), ("st", 1)], 5: [("sc", 0), ("sc", 1)]}
    for kc in range(8):
        for half in range(3):
            slot = (kc * 3 + half) % 2
            ld("sp", f"wm{slot}", wmst[slot][:], wmod_d[kc * 128:(kc + 1) * 128, half * 1024:(half + 1) * 1024], [("wmst", slot)])
            for j in range(2):
                cg = half * 2 + j
                P.op("pe", lambda e, kc=kc, cg=cg, j=j, slot=slot: e.matmul(
                    banks6[cg][0:NS, :], lhsT=cT[:, kc, :], rhs=wmst[slot][:, j * 512:(j + 1) * 512],
                    start=(kc == 0), stop=(kc == 7)), reads=["cT", ("wmst", slot)], writes=bkeys_full[cg])
    for cg in range(6):
        dst = (mod_sh, mod_sh, mod_sc, mod_sc, modsb_g, modsb_g)[cg]
        badd = (bmod3[:, cg * 512:(cg + 1) * 512] if cg < 4 else gg[0:NS, (cg - 4) * 512:(cg - 3) * 512])
        P.op("dve", lambda e, cg=cg, dst=dst, badd=badd: e.tensor_tensor(out=dst[:, (cg % 2) * 512:(cg % 2 + 1) * 512], in0=banks6[cg][0:NS, :],
                                                                       in1=badd, op=ALU.add),
             reads=bkeys_full[cg] + ["bmod3", "bgate3"], writes=[("modsb", cg)])
    for which in range(2):
        for fc in range(8):
            col = which * D + fc * 128
            msrc = (mod_sh, mod_sc)[which]
            P.op("pe", lambda e, which=which, fc=fc, msrc=msrc: e.transpose(
                out=ps_pv[0][:, (which * 8 + fc) * NS:(which * 8 + fc + 1) * NS], in_=msrc[:, fc * 128:(fc + 1) * 128],
                identity=ident_f[0:NS, 0:NS]), reads=[("modsb", col // 512), "ident_f"], writes=[("pv", 0)])
    for s in range(NS):
        P.op("dve", lambda e, s=s: e.tensor_copy(
            out=shf[:, s, :], in_=ps_pv[0][:, 0:8 * NS].rearrange("p (f s) -> p s f", s=NS)[:, s, :]),
            reads=[("pv", 0)], writes=["shf"])
        P.op("dve", lambda e, s=s: e.scalar_tensor_tensor(
            out=gs[:, s, :], in0=ps_pv[0][:, 8 * NS:16 * NS].rearrange("p (f s) -> p s f", s=NS)[:, s, :],
            scalar=1.0, in1=vecs[:, V_GPRE:V_GPRE + 8], op0=ALU.add, op1=ALU.mult),
            reads=[("pv", 0), "vecs"], writes=["gs"])
    ld("sp", "gst", gate_d[:, :], modsb_g, [], rkeys=[("modsb", 4), ("modsb", 5)])
    P.barrier()

    state = {"wslot": 0, "xslot": 0, "ost": 0, "tile": 0, "sbank": 0, "nb": 0, "tht": 0}
    dve_act = ["dve", "act"]

    def load_gg(s):
        tmpk = [("t1", 0), ("t1", 1), ("t2", 0), ("t2", 1)]
        tmp = tt4[:, :, :].rearrange("p c t -> p (c t)")
        ld("pool", "ggl", gg[:, :], gate_d[s:s + 1, :].partition_broadcast(128), [("gg", 0), ("gg", 1)])
        ld("pool", "ggl", tmp, gpost_d[0:1, :].partition_broadcast(128), tmpk)
        P.op("dve", lambda e: e.tensor_tensor(out=gg[:, :], in0=gg[:, :], in1=tmp, op=ALU.mult),
             reads=tmpk, writes=[("gg", 0), ("gg", 1)])

    def issue_x_load(x_ap, ntok, slot):
        if ntok == T:
            ld("pool", f"x{slot}", xin[slot][:, :, :], x_ap.rearrange("(t p) d -> p t d", p=128), [("xin", slot, 0), ("xin", slot, 1)])
        else:
            ld("pool", f"x{slot}", xin[slot][0:ntok, 0, :], x_ap, [("xin", slot, 0)])

    def tile(s, x_slot, ntok, c0, first, y_ap, kv_out=None, u_out=None, pre_mid=None):
        cur = ["prenorm"]
        TT = (ntok + 127) // 128
        rows = [min(128, ntok - tt * 128) for tt in range(TT)]
        nq = (ntok + 63) // 64
        cur[0] = "prenorm"; P.phase = "prenorm"
        for tt in range(TT):
            r = rows[tt]
            xt = xin[x_slot][0:r, tt, :]
            P.op("act", lambda e, r=r, xt=xt, tt=tt: e.activation(out=sqj[0:r, :], in_=xt, func=AF.Square, accum_out=st8[0:r, tt:tt + 1]),
                 reads=[("xin", x_slot, tt)], writes=["sqj", ("st8", tt)])
            P.op("dve", lambda e, r=r, tt=tt: e.tensor_scalar(out=st8[0:r, 2 + tt:3 + tt], in0=st8[0:r, tt:tt + 1], scalar1=1.0 / D, scalar2=EPS,
                                                              op0=ALU.mult, op1=ALU.add), reads=[("st8", tt)], writes=[("st8", 2 + tt)])
            P.op("pool", lambda e, r=r, tt=tt: e.tensor_tensor(out=st8[0:r, 6 + tt:7 + tt], in0=st8[0:r, 2 + tt:3 + tt], in1=mhalf[0:r, :], op=ALU.pow),
                 reads=[("st8", 2 + tt)], writes=[("st8", 6 + tt)])
            P.op("dve", lambda e, r=r, xt=xt, tt=tt: e.tensor_scalar(out=xsb[tt][0:r, :], in0=xt, scalar1=st8[0:r, 6 + tt:7 + tt], scalar2=None,
                                                                     op0=ALU.mult), reads=[("xin", x_slot, tt), ("st8", 6 + tt)], writes=[("xsb", tt)])
            xbf = (ps_st_bf, ps_mm1_bf)[tt % 2]
            xbk = [("st", 0), ("st", 1)] if tt % 2 == 0 else [("mm", 1, 0), ("mm", 1, 1)]
            for fc in range(8):
                P.op("pe", lambda e, r=r, tt=tt, fc=fc, xbf=xbf: e.transpose(out=xbf[:, fc * 128:fc * 128 + r], in_=xsb[tt][0:r, fc * 128:(fc + 1) * 128],
                                                                             identity=ident[0:r, 0:r]),
                     reads=[("xsb", tt)], writes=xbk)
            xv = xbf[:, :].rearrange("p (f t) -> p f t", f=8)[:, :, 0:r]
            hv = hT[:, :, tt * 128:tt * 128 + r]
            hk = [("hT", fc, tt) for fc in range(8)]
            P.op("dve", lambda e, r=r, xv=xv, hv=hv: e.tensor_tensor(out=hv, in0=xv, in1=gs[:, s, :].unsqueeze(2).broadcast_to([128, 8, r]), op=ALU.mult),
                 reads=xbk, writes=hk)
            P.op("dve", lambda e, r=r, hv=hv: e.tensor_tensor(out=hv, in0=hv, in1=shf[:, s, :].unsqueeze(2).broadcast_to([128, 8, r]), op=ALU.add),
                 reads=hk, writes=hk)
        hT_keys = [("hT", fc, tt) for fc in range(8) for tt in range(TT)]

        yield "h1"

        P.phase = cur[0]
        cur[0] = "inproj"; P.phase = "inproj"
        mmslot = [0]

        def load_group(g):
            ws = state["wslot"]; state["wslot"] = (ws + 1) % NW
            ld("sp", f"w{ws}", wst[ws][:].rearrange("p k n -> p (k n)"), wbf_d[g], [("wst", ws)], rkeys=[("wbf", g)])
            return ws

        def fm_pair(ws, p, evac):
            sl = mmslot[0]; mmslot[0] = (sl + 1) % 2
            pk = [("mm", sl, 0), ("mm", sl, 1)]
            for half in range(2):
                c = 2 * p + half
                pa = ps_mm[sl][:, half * 256:half * 256 + ntok]
                for kc in range(8):
                    P.op("pe", lambda e, pa=pa, ws=ws, kc=kc, c=c: e.matmul(pa, lhsT=wst[ws][:, kc, c * 128:(c + 1) * 128], rhs=hT[:, kc, 0:ntok],
                                                                            start=(kc == 0), stop=(kc == 7)),
                         reads=[("wst", ws)] + hT_keys, writes=pk)
            bv = ps_mm[sl][:, :].rearrange("p (h t) -> p h t", h=2)[:, :, 0:ntok]
            evac(2 * p, bv, pk)

        def fm_group(g, evac):
            ws = load_group(g)
            for p in range(2):
                fm_pair(ws, p, evac)
            return ws

        def tm_group(g, evac, ws=None):
            if ws is None:
                ws = load_group(g)
            for tt in range(TT):
                r = rows[tt]
                bk = tt % 2
                pa = ps_mm[bk][0:r, :]
                pk = [("mm", bk, 0), ("mm", bk, 1)]
                for kc in range(8):
                    P.op("pe", lambda e, pa=pa, ws=ws, kc=kc, tt=tt, r=r: e.matmul(pa, lhsT=hT[:, kc, tt * 128:tt * 128 + r], rhs=wst[ws][:, kc, :],
                                                                                   start=(kc == 0), stop=(kc == 7)),
                         reads=[("wst", ws)] + hT_keys, writes=pk)
                evac(tt, r, pa, pk)
            mmslot[0] = 0
            return ws

        def out_stage(src_evac, dst_ap, r0, r1):
            so = state["ost"]; state["ost"] = (so + 1) % 2
            src_evac(ostage[so], ("ostage", so))
            ld("pool", f"os{so}", dst_ap, ostage[so][r0:r1, :], [], rkeys=[("ostage", so)])

        def silu2_fm(dst, key):
            def ev(c, pa, pk):
                P.op("act", lambda e: e.activation(out=tht[:, :, 0:ntok], in_=pa, func=AF.Tanh, scale=0.5), reads=pk, writes=["tht"])
                P.op("dve", lambda e: e.scalar_tensor_tensor(out=dst[:, c:c + 2, 0:ntok], in0=tht[:, :, 0:ntok], scalar=1.0, in1=pa, op0=ALU.add, op1=ALU.mult),
                     reads=pk + ["tht"], writes=[(key, c), (key, c + 1)])
            return ev

        def ev_b(c, pa, pk):
            P.op("act", lambda e: e.activation(out=sigb[:, c:c + 2, 0:ntok], in_=pa, func=AF.Tanh, scale=0.5), reads=pk, writes=[("sigb", c), ("sigb", c + 1)])
            P.op("dve", lambda e: e.tensor_scalar(out=sigb[:, c:c + 2, 0:ntok], in0=sigb[:, c:c + 2, 0:ntok], scalar1=0.5, scalar2=0.5, op0=ALU.mult, op1=ALU.add),
                 reads=[("sigb", c), ("sigb", c + 1)], writes=[("sigb", c), ("sigb", c + 1)])
        fm_group(1, ev_b)
        yield "h2"
        P.phase = cur[0]
        if first:
            P.op("dve", lambda e: e.memset(uring[:, :, 0:30], 0.0), writes=["uhist"])

        def ev_a(c, pa, pk):
            P.op("dve", lambda e: e.tensor_tensor(out=uring[:, c:c + 2, 30:30 + ntok], in0=pa, in1=sigb[:, c:c + 2, 0:ntok], op=ALU.mult),
                 reads=pk + [("sigb", c), ("sigb", c + 1)], writes=[("u", c), ("u", c + 1)])
        fm_group(0, ev_a)

        if u_out is not None:
            cur[0] = "uout"; P.phase = "uout"
            ttu = TT - 1; ru = rows[ttu]
            pk0 = [("mm", 0, 0), ("mm", 0, 1)]; pk1 = [("mm", 1, 0), ("mm", 1, 1)]
            pbu = ps_mm[0][0:ru, :]; pau = ps_mm[1][0:ru, :]
            wsb = load_group(1)
            for kc in range(8):
                P.op("pe", lambda e, kc=kc, pbu=pbu, ttu=ttu, ru=ru, wsb=wsb: e.matmul(
                    pbu, lhsT=hT[:, kc, ttu * 128:ttu * 128 + ru], rhs=wst[wsb][:, kc, :], start=(kc == 0), stop=(kc == 7)),
                    reads=[("wst", wsb)] + hT_keys, writes=pk0)
            P.op("act", lambda e, pbu=pbu, ru=ru: e.activation(out=kstage[0:ru, :], in_=pbu, func=AF.Tanh, scale=0.5), reads=pk0, writes=["kstage"])
            wsa = load_group(0)
            for kc in range(8):
                P.op("pe", lambda e, kc=kc, pau=pau, ttu=ttu, ru=ru, wsa=wsa: e.matmul(
                    pau, lhsT=hT[:, kc, ttu * 128:ttu * 128 + ru], rhs=wst[wsa][:, kc, :], start=(kc == 0), stop=(kc == 7)),
                    reads=[("wst", wsa)] + hT_keys, writes=pk1)

            def ev_u(o, ok, pau=pau, ru=ru):
                P.op("dve", lambda e: e.scalar_tensor_tensor(out=o[0:ru, :], in0=kstage[0:ru, :], scalar=1.0, in1=pau, op0=ALU.add, op1=ALU.mult),
                     reads=pk1 + ["kstage"], writes=[ok])
                P.op("dve", lambda e: e.tensor_scalar(out=o[0:ru, :], in0=o[0:ru, :], scalar1=0.5, scalar2=None, op0=ALU.mult), reads=[ok], writes=[ok])
            out_stage(ev_u, u_out, ru - 30, ru)
            mmslot[0] = 0
            cur[0] = "inproj"; P.phase = "inproj"

        def conv_chunk(c):
            ph = P.phase; P.phase = "conv"
            pc = (ps_sc, ps_st)[c % 2][:, 0:ntok]
            cvk = [(("sc", "st")[c % 2], 0), (("sc", "st")[c % 2], 1)]
            for k in range(31):
                P.op("pe", lambda e, pc=pc, c=c, k=k: e.matmul(pc, lhsT=diag[:, c * 31 + k, :], rhs=uring[:, c, k:k + ntok], start=(k == 0), stop=(k == 30)),
                     reads=[("u", c), "uhist"], writes=cvk)
            P.op("dve", lambda e, pc=pc, c=c: e.tensor_scalar(out=ycv[:, c, 0:ntok], in0=pc, scalar1=vecs[:, V_DWB + c:V_DWB + c + 1], scalar2=None, op0=ALU.add),
                 reads=cvk, writes=[("ycv", c)])
            P.op("act", lambda e, c=c: e.activation(out=ysq[:, c, 0:ntok], in_=ycv[:, c, 0:ntok], func=AF.Square),
                 reads=[("ycv", c)], writes=[("ysq", c)])
            P.op("dve", lambda e, c=c: e.tensor_copy(out=ybf[:, c, 0:ntok], in_=ycv[:, c, 0:ntok]), reads=[("ycv", c)], writes=[("ybf", c)])
            P.phase = ph

        yield "h3"

        P.phase = cur[0]
        fm_group(2, silu2_fm(szc, "szc"))
        yield "head_done"
        P.phase = cur[0]
        if pre_mid is not None:
            pre_mid()
        conv_chunk(0)

        def ev_q(c, pa, pk):
            P.op("dve", lambda e: e.tensor_copy(out=qT[:, c:c + 2, 0:ntok], in_=pa), reads=pk, writes=[("qT", c), ("qT", c + 1)])
        fm_group(3, ev_q)
        conv_chunk(1)

        kcol = (c0 % 12) * 64

        def ev_k(c, pa, pk):
            P.op("act", lambda e: e.activation(out=kring[:, c:c + 2, kcol:kcol + ntok], in_=pa, func=AF.Copy), reads=pk,
                 writes=[("k", cc, (c0 % 12) // 2 + j) for cc in (c, c + 1) for j in range((ntok + 127) // 128)])
        ws_k = fm_group(4, ev_k)
        if kv_out is not None:
            def ev_ktm(tt, r, pa, pk):
                dst = kv_out[0](tt, r)
                if dst is None:
                    return
                out_stage(lambda o, ok: P.op("act", lambda e: e.activation(out=o[0:r, :], in_=pa, func=AF.Copy), reads=pk, writes=[ok]), dst, 0, r)
            tm_group(4, ev_ktm, ws=ws_k)
        conv_chunk(2)

        a0 = c0 // 2

        def ev_v(tt, r, pa, pk):
            vs = (a0 + tt) % 6
            dst = kv_out[1](tt, r) if kv_out is not None else None
            if dst is None:
                P.op("dve", lambda e: e.tensor_copy(out=vaug[vs][0:r, :, 0:64], in_=pa.rearrange("p (h d) -> p h d", d=64)),
                     reads=pk, writes=[("v", vs)])
            else:
                so = state["ost"]; state["ost"] = (so + 1) % 2
                P.op("act", lambda e: e.activation(out=ostage[so][0:r, :], in_=pa, func=AF.Copy), reads=pk, writes=[("ostage", so)])
                P.op("dve", lambda e: e.tensor_copy(out=vaug[vs][0:r, :, 0:64], in_=ostage[so][0:r, :].rearrange("p (h d) -> p h d", d=64)),
                     reads=[("ostage", so)], writes=[("v", vs)])
                ld("pool", f"os{so}", dst, ostage[so][0:r, :], [], rkeys=[("ostage", so)])
        tm_group(5, ev_v)
        conv_chunk(3)

        cur[0] = "conv"; P.phase = "conv"
        P.op("dve", lambda e: e.tensor_copy(out=uring[:, :, 0:30], in_=uring[:, :, ntok:ntok + 30]),
             reads=[("u", c) for c in range(4)], writes=["uhist"])
        pmean = ps_st[:, 0:ntok]; pmsq = ps_st[:, 256:256 + ntok]
        for c in range(4):
            P.op("pe", lambda e, c=c: e.matmul(pmean, lhsT=ones_m[:], rhs=ybf[:, c, 0:ntok], start=(c == 0), stop=(c == 3)),
                 reads=[("ybf", c)], writes=[("st", 0), ("st", 1)])
        for c in range(4):
            P.op("pe", lambda e, c=c: e.matmul(pmsq, lhsT=ones_m[:], rhs=ysq[:, c, 0:ntok], start=(c == 0), stop=(c == 3)),
                 reads=[("ysq", c)], writes=[("st", 0), ("st", 1)])
        P.op("act", lambda e: e.activation(out=mean_sb[:, 0:ntok], in_=pmean, func=AF.Copy), reads=[("st", 0), ("st", 1)], writes=["mean_sb"])
        P.op("dve", lambda e: e.tensor_tensor(out=m2[:, 0:ntok], in0=mean_sb[:, 0:ntok], in1=mean_sb[:, 0:ntok], op=ALU.mult), reads=["mean_sb"], writes=["m2"])
        P.op("dve", lambda e: e.tensor_tensor(out=m2[:, 0:ntok], in0=pmsq, in1=m2[:, 0:ntok], op=ALU.subtract), reads=[("st", 0), ("st", 1), "m2"], writes=["m2"])
        P.op("dve", lambda e: e.tensor_scalar(out=m2[:, 0:ntok], in0=m2[:, 0:ntok], scalar1=0.0, scalar2=EPS, op0=ALU.max, op1=ALU.add), reads=["m2"], writes=["m2"])
        P.op("act", lambda e: e.activation(out=m2[:, 0:ntok], in_=m2[:, 0:ntok], func=AF.Sqrt), reads=["m2"], writes=["m2"])
        P.op("dve", lambda e: e.reciprocal(out=lrstd[:, 0:ntok], in_=m2[:, 0:ntok]), reads=["m2"], writes=["lrstd"])
        def ln_tail():
            cur[0] = "conv"; P.phase = "conv"
            allc = [("ycv", c) for c in range(4)]
            tk4 = [("t1", 0), ("t1", 1), ("t2", 0), ("t2", 1)]
            mean_b = mean_sb[:, 0:ntok].unsqueeze(1).broadcast_to([128, 4, ntok])
            rstd_b = lrstd[:, 0:ntok].unsqueeze(1).broadcast_to([128, 4, ntok])
            P.op("dve", lambda e: e.tensor_tensor(out=ycv[:, :, 0:ntok], in0=ycv[:, :, 0:ntok], in1=mean_b, op=ALU.subtract),
                 reads=allc + ["mean_sb"], writes=allc)
            P.op("dve", lambda e: e.tensor_tensor(out=ycv[:, :, 0:ntok], in0=ycv[:, :, 0:ntok], in1=rstd_b, op=ALU.mult),
                 reads=allc + ["lrstd"], writes=allc)
            for c in range(4):
                P.op("dve", lambda e, c=c: e.tensor_scalar(out=ycv[:, c, 0:ntok], in0=ycv[:, c, 0:ntok], scalar1=vecs[:, V_LNG + c:V_LNG + c + 1],
                                                           scalar2=vecs[:, V_LNB + c:V_LNB + c + 1], op0=ALU.mult, op1=ALU.add),
                     reads=[("ycv", c)], writes=[("ycv", c)])
            cur[0] = "attn"; P.phase = "attn"

        def ln_tail_b():
            cur[0] = "conv"; P.phase = "conv"
            allc = [("ycv", c) for c in range(4)]
            tk4 = [("t1", 0), ("t1", 1), ("t2", 0), ("t2", 1)]
            P.op("act", lambda e: e.activation(out=tt4[:, :, 0:ntok], in_=ycv[:, :, 0:ntok], func=AF.Tanh, scale=0.5), reads=allc, writes=tk4)
            P.op("dve", lambda e: e.scalar_tensor_tensor(out=tt4[:, :, 0:ntok], in0=tt4[:, :, 0:ntok], scalar=1.0, in1=ycv[:, :, 0:ntok], op0=ALU.add, op1=ALU.mult),
                 reads=allc + tk4, writes=tk4)
            P.op("dve", lambda e: e.scalar_tensor_tensor(out=cin[:, :, 0:ntok], in0=tt4[:, :, 0:ntok], scalar=0.25, in1=szc[:, :, 0:ntok], op0=ALU.mult, op1=ALU.mult),
                 reads=tk4 + [("szc", c) for c in range(4)], writes=[("cin", c) for c in range(4)])
            cur[0] = "attn"; P.phase = "attn"

        def ev_gate(dstg, keyg, off):
            def ev(c, pa, pk):
                P.op("act", lambda e: e.activation(out=dstg[:, off + c:off + c + 2, 0:ntok], in_=pa, func=AF.Tanh, scale=0.5), reads=pk,
                     writes=[(keyg, off + c), (keyg, off + c + 1)])
            return ev
        late = {0: [(6, silu2_fm(sza, "sza"))], 1: [(7, ev_gate(sgc, "sgc", 0))], 2: [(8, ev_gate(sgc, "sgc", 4))],
                3: [(9, ev_gate(sga, "sga", 0))], 4: [(10, ev_gate(sga, "sga", 4))]}

        cur[0] = "attn"; P.phase = "attn"
        kcur_end = c0 * 64 + ntok
        tiles = []
        for t in range(6):
            a = a0 - 4 + t
            if a < 0:
                continue
            nk = min(128, kcur_end - a * 128)
            if nk <= 0:
                continue
            tiles.append((t, a, nk))
        QB = (ntok + 127) // 128
        s_banks = [(ps_sc, "sc", 0), (ps_st, "st", 0), (ps_cv[0], "cv", 0), (ps_cv[1], "cv", 2)]
        for h in range(8):
            if h == 6:
                yield "mid_done"
                P.phase = cur[0]
            if h == 7:
                yield "t1"
                P.phase = cur[0]
            j, hp = h // 2, h % 2
            par = h % 2
            prow = slice(hp * 64, hp * 64 + 64)
            for (t, a, nk) in tiles:
                i_lo = max(0, 2 * t - 8); i_hi = min(nq - 1, 2 * t + 1)
                if i_lo > i_hi:
                    continue
                c_lo = i_lo * 64; c_hi = min((i_hi + 1) * 64, ntok)
                sbk = s_banks[state["sbank"] % 4]; state["sbank"] += 1
                psS = sbk[0][:, 0:256]
                sck = [(sbk[1], sbk[2]), (sbk[1], sbk[2] + 1)]
                kph = (a % 6) * 128
                near_t = max(0, 2 * t - 8) <= min(nq - 1, 2 * t - 5)
                P.op("pe", lambda e, psS=psS, nk=nk, c_lo=c_lo, c_hi=c_hi, j=j, prow=prow, kph=kph, near_t=near_t: e.matmul(
                    psS[0:nk, c_lo:c_hi], lhsT=kring[prow, j, kph:kph + nk], rhs=qT[prow, j, c_lo:c_hi], start=True, stop=(not near_t)),
                    reads=[("k", j, a % 6), ("qT", j)], writes=sck)
                ptile = PT[par][t]
                pkey = ("PT", par, t)
                n_lo = max(0, 2 * t - 8); n_hi = min(nq - 1, 2 * t - 5)
                has_near = n_lo <= n_hi
                if has_near:
                    q0 = n_lo * 64; q1 = min((n_hi + 1) * 64, ntok); w = q1 - q0
                    b0 = (8 + n_lo - 2 * t) * 64
                    for bi, btile in enumerate((BThi, BTlo)):
                        P.op("pe", lambda e, psS=psS, nk=nk, q0=q0, q1=q1, w=w, b0=b0, h=h, btile=btile, bi=bi: e.matmul(
                            psS[0:nk, q0:q1], lhsT=ident[0:nk, 0:nk], rhs=btile[0:nk, h, b0:b0 + w], start=False, stop=(bi == 1)),
                            reads=[], writes=sck)
                f_lo = max(0, 2 * t - 4); f_hi = min(nq - 1, 2 * t)
                has_far = f_lo <= f_hi
                if has_near or has_far:
                    e_lo = n_lo if has_near else f_lo
                    e_hi = f_hi if has_far else n_hi
                    q0 = e_lo * 64; q1 = min((e_hi + 1) * 64, ntok)
                    P.op("act", lambda e, ptile=ptile, psS=psS, nk=nk, q0=q0, q1=q1: e.activation(
                        out=ptile[0:nk, q0:q1], in_=psS[0:nk, q0:q1], func=AF.Exp, scale=0.125), reads=sck, writes=[pkey])
                ie = 2 * t + 1
                if ie <= nq - 1 and ie >= max(0, 2 * t - 4) and nk > 64:
                    q0 = ie * 64; q1 = min((ie + 1) * 64, ntok)
                    P.op("act", lambda e, ptile=ptile, psS=psS, nk=nk, q0=q0, q1=q1: e.activation(
                        out=ptile[64:nk, q0:q1], in_=psS[64:nk, q0:q1], func=AF.Exp, scale=0.125), reads=sck, writes=[pkey])
            for (g, ev) in late.get(h, ()):
                cur[0] = "inproj"; P.phase = "inproj"
                fm_group(g, ev)
                cur[0] = "attn"; P.phase = "attn"
            for qb in range(QB):
                r = rows[qb]
                bank = ps_pv[qb]
                hh = h % 4
                use = [(t, a, nk) for (t, a, nk) in tiles if qb <= t <= qb + 4]
                for idx, (t, a, nk) in enumerate(use):
                    P.op("pe", lambda e, bank=bank, hh=hh, r=r, qb=qb, t=t, a=a, nk=nk, idx=idx, nuse=len(use), par=par, h=h: e.matmul(
                        bank[0:r, hh * 65:hh * 65 + 65],
                        lhsT=PT[par][t][0:nk, qb * 128:qb * 128 + r], rhs=vaug[a % 6][0:nk, h, :], start=(idx == 0), stop=(idx == nuse - 1)),
                        reads=[("PT", par, t), ("v", a % 6), ("vaug1", a % 6)], writes=[("pv", qb)])
                P.op("dve", lambda e, bank=bank, hh=hh, r=r, h=h: e.reciprocal(out=rc[0:r, h:h + 1], in_=bank[0:r, hh * 65 + 64:hh * 65 + 65]),
                     reads=[("pv", qb)], writes=[("rc", h)])
                P.op("dve", lambda e, bank=bank, hh=hh, r=r, h=h, qb=qb: e.tensor_scalar(
                    out=onorm[qb][0:r, h * 64:(h + 1) * 64], in0=bank[0:r, hh * 65:hh * 65 + 64], scalar1=rc[0:r, h:h + 1], scalar2=None, op0=ALU.mult),
                    reads=[("pv", qb), ("rc", h)], writes=[("onorm", qb, h)])
            if h == 0:
                ln_tail()
            if h == 1:
                ln_tail_b()
        yield "t2"
        P.phase = cur[0]
        for qb in range(QB):
            r = rows[qb]
            obf = (ps_st_bf, ps_sc_bf)[qb % 2]
            obk = [(("st", "sc")[qb % 2], 0), (("st", "sc")[qb % 2], 1)]
            for fc in range(4):
                P.op("pe", lambda e, qb=qb, r=r, fc=fc, obf=obf: e.transpose(out=obf[:, fc * 128:fc * 128 + r], in_=onorm[qb][0:r, fc * 128:(fc + 1) * 128],
                                                                             identity=ident[0:r, 0:r]),
                     reads=[("onorm", qb, 2 * fc), ("onorm", qb, 2 * fc + 1)], writes=obk)
            for fc in range(4):
                P.op("dve", lambda e, qb=qb, r=r, fc=fc, obf=obf: e.scalar_tensor_tensor(out=oT[:, fc, qb * 128:qb * 128 + r], in0=obf[:, fc * 128:fc * 128 + r],
                                                                                         scalar=0.5, in1=sza[:, fc, qb * 128:qb * 128 + r], op0=ALU.mult, op1=ALU.mult),
                     reads=obk + [("sza", fc)], writes=[("oT", fc, qb)])

        yield "t3"

        P.phase = cur[0]
        cur[0] = "outproj"; P.phase = "outproj"
        for fp in range(4):
            fo0 = 2 * fp
            bka = (ps_mm[0], ps_mm[1], ps_cv[0], ps_cv[1])[fp]
            pak = ([("mm", 0, 0), ("mm", 0, 1)], [("mm", 1, 0), ("mm", 1, 1)], [("cv", 0), ("cv", 1)], [("cv", 2), ("cv", 3)])[fp]
            bkb = (ps_sc, ps_st, ps_pv[0], ps_pv[1])[fp]
            pbk = ([("sc", 0), ("sc", 1)], [("st", 0), ("st", 1)], [("pv", 0)], [("pv", 1)])[fp]
            for half in range(2):
                fo = fo0 + half
                pa = bka[:, half * 256:half * 256 + ntok]
                pb = bkb[:, half * 256:half * 256 + ntok]
                for kc in range(4):
                    P.op("pe", lambda e, pa=pa, kc=kc, fo=fo: e.matmul(pa, lhsT=wco[:, kc, fo * 128:(fo + 1) * 128], rhs=cin[:, kc, 0:ntok], start=(kc == 0), stop=(kc == 3)),
                         reads=[("cin", kc)], writes=pak)
                for kc in range(4):
                    P.op("pe", lambda e, pb=pb, kc=kc, fo=fo: e.matmul(pb, lhsT=wao[:, kc, fo * 128:(fo + 1) * 128], rhs=oT[:, kc, 0:ntok], start=(kc == 0), stop=(kc == 3)),
                         reads=[("oT", kc, qb) for qb in range(QB)], writes=pbk)
            va = bka[:, :].rearrange("p (h t) -> p h t", h=2)[:, :, 0:ntok]
            vb = bkb[:, :].rearrange("p (h t) -> p h t", h=2)[:, :, 0:ntok]
            P.op("dve", lambda e, va=va, fo0=fo0: e.scalar_tensor_tensor(out=mt2[0][:, :, 0:ntok], in0=sgc[:, fo0:fo0 + 2, 0:ntok], scalar=1.0, in1=va, op0=ALU.add, op1=ALU.mult),
                 reads=pak + [("sgc", fo0), ("sgc", fo0 + 1)], writes=[("mt", 0)])
            P.op("dve", lambda e, vb=vb, fo0=fo0: e.scalar_tensor_tensor(out=mt2[1][:, :, 0:ntok], in0=sga[:, fo0:fo0 + 2, 0:ntok], scalar=1.0, in1=vb, op0=ALU.add, op1=ALU.mult),
                 reads=pbk + [("sga", fo0), ("sga", fo0 + 1)], writes=[("mt", 1)])
            P.op("dve", lambda e, fo0=fo0: e.tensor_tensor(out=merged[:, fo0:fo0 + 2, 0:ntok], in0=mt2[0][:, :, 0:ntok], in1=mt2[1][:, :, 0:ntok], op=ALU.add),
                 reads=[("mt", 0), ("mt", 1)], writes=[("merged", fo0), ("merged", fo0 + 1)])

        yield "t4"

        P.phase = cur[0]
        cur[0] = "wo"; P.phase = "wo"
        def wok(tt, hf):
            return [("cv", 2 * hf), ("cv", 2 * hf + 1)] if tt % 2 == 0 else [("mm", hf, 0), ("mm", hf, 1)]
        for tt in range(TT):
            r = rows[tt]
            for hf in range(2):
                po = (ps_cv, ps_mm)[tt % 2][hf][0:r, :]
                for kc in range(8):
                    P.op("pe", lambda e, po=po, kc=kc, tt=tt, r=r, hf=hf: e.matmul(po, lhsT=merged[:, kc, tt * 128:tt * 128 + r], rhs=wo[:, kc, hf * 512:(hf + 1) * 512],
                                                                                   start=(kc == 0), stop=(kc == 7)),
                         reads=[("merged", kc)], writes=wok(tt, hf))
                P.op("act", lambda e, po=po, r=r, hf=hf: e.activation(out=sqj[0:r, 0:512], in_=po, func=AF.Square, accum_out=st8[0:r, hf:hf + 1]),
                     reads=wok(tt, hf), writes=["sqj", ("st8", hf)])
            P.op("dve", lambda e, r=r: e.tensor_tensor(out=st8[0:r, 2:3], in0=st8[0:r, 0:1], in1=st8[0:r, 1:2], op=ALU.add),
                 reads=[("st8", 0), ("st8", 1)], writes=[("st8", 2)])
            P.op("dve", lambda e, r=r: e.tensor_scalar(out=st8[0:r, 3:4], in0=st8[0:r, 2:3], scalar1=1.0 / D, scalar2=4.0 * EPS, op0=ALU.mult, op1=ALU.add),
                 reads=[("st8", 2)], writes=[("st8", 3)])
            P.op("pool", lambda e, r=r: e.tensor_tensor(out=st8[0:r, 6:7], in0=st8[0:r, 3:4], in1=mhalf[0:r, :], op=ALU.pow), reads=[("st8", 3)], writes=[("st8", 6)])
            for hf in range(2):
                po = (ps_cv, ps_mm)[tt % 2][hf][0:r, :]
                P.op("dve", lambda e, po=po, r=r, hf=hf: e.scalar_tensor_tensor(out=ytmp[hf][0:r, :], in0=po, scalar=st8[0:r, 6:7], in1=gg[0:r, hf * 512:(hf + 1) * 512],
                                                                                op0=ALU.mult, op1=ALU.mult),
                     reads=wok(tt, hf) + [("st8", 6), ("gg", hf)], writes=[(("t1", "t2")[hf], 0), (("t1", "t2")[hf], 1)])
                P.op("dve", lambda e, r=r, hf=hf, tt=tt: e.tensor_tensor(out=xin[x_slot][0:r, tt, hf * 512:(hf + 1) * 512], in0=xin[x_slot][0:r, tt, hf * 512:(hf + 1) * 512],
                                                                          in1=ytmp[hf][0:r, :], op=ALU.add),
                     reads=[(("t1", "t2")[hf], 0), (("t1", "t2")[hf], 1), ("xin", x_slot, tt)], writes=[("xin", x_slot, tt)])
        if ntok == T:
            ld("pool", f"yo{x_slot}", y_ap.rearrange("(t p) d -> p t d", p=128), xin[x_slot][:, :, :], [], rkeys=[("xin", x_slot, 0), ("xin", x_slot, 1)])
        else:
            ld("pool", f"yo{x_slot}", y_ap, xin[x_slot][0:ntok, 0, :], [], rkeys=[("xin", x_slot, 0)])

    s_samp = nseq
    issue_x_load(xs_d[:, :], DEC_T, 0)
    for i in range(4):
        a = 4 + i
        ld("sp", "ldk", kstage[:], ck_d[i * 128:(i + 1) * 128, :], ["kstage"])
        P.op("dve", lambda e: e.tensor_copy(out=kstage_b[:], in_=kstage[:]), reads=["kstage"], writes=["kstage_b"])
        for j in range(4):
            P.op("pe", lambda e, j=j: e.transpose(out=ps_st_bf[:, j * 128:(j + 1) * 128], in_=kstage_b[:, j * 128:(j + 1) * 128], identity=ident[:]),
                 reads=["kstage_b"], writes=[("st", 0), ("st", 1)])
        P.op("act", lambda e, a=a: e.activation(out=kring[:, :, (a % 6) * 128:(a % 6) * 128 + 128],
                                                in_=ps_st_bf[:, 0:512].rearrange("p (j t) -> p j t", j=4), func=AF.Copy),
             reads=[("st", 0), ("st", 1)], writes=[("k", j, a % 6) for j in range(4)])
        ld("sp", f"ldv{i}", ostage[i % 2][:], cv_d[i * 128:(i + 1) * 128, :], [("ostage", i % 2)])
        P.op("dve", lambda e, a=a, i=i: e.tensor_copy(out=vaug[a % 6][:, :, 0:64], in_=ostage[i % 2][:].rearrange("p (h d) -> p h d", d=64)),
             reads=[("ostage", i % 2)], writes=[("v", a % 6)])
    ld("sp", "ldk", kstage[0:30, :], cconv_d[:, :], ["kstage"])
    P.op("dve", lambda e: e.tensor_copy(out=kstage_b[0:30, :], in_=kstage[0:30, :]), reads=["kstage"], writes=["kstage_b"])
    for c in range(4):
        P.op("pe", lambda e, c=c: e.transpose(out=ps_st_bf[:, c * 128:c * 128 + 30], in_=kstage_b[0:30, c * 128:(c + 1) * 128], identity=ident[0:30, 0:30]),
             reads=["kstage_b"], writes=[("st", 0), ("st", 1)])
    P.op("act", lambda e: e.activation(out=uring[:, :, 0:30], in_=ps_st_bf[:, 0:512].rearrange("p (c t) -> p c t", c=4)[:, :, 0:30], func=AF.Copy),
         reads=[("st", 0), ("st", 1)], writes=["uhist"])
    ld("sp", "cpy", nks_d[0:480, :], ck_d[32:512, :], [])
    ld("sp", "cpy", nvs_d[0:480, :], cv_d[32:512, :], [])
    descs = [dict(s=s_samp, ntok=DEC_T, c0=PAST // 64, first=False, y=ys_d[:, :], x=xs_d[:, :],
                  kv=(lambda tt, r: nks_d[480:512, :], lambda tt, r: nvs_d[480:512, :]), u=ncs_d[:, :],
                  pre_mid=(lambda: load_gg(s_samp)))]
    for b in range(nseq):
        for n in range(NT):
            tok0 = n * T
            kvo = None
            if tok0 + T > seqlen - WP:
                def kdst(tt, r, b=b, tok0=tok0):
                    p0 = tok0 + tt * 128 - (seqlen - WP)
                    return nkp_d[b, p0:p0 + r, :] if p0 >= 0 else None

                def vdst(tt, r, b=b, tok0=tok0):
                    p0 = tok0 + tt * 128 - (seqlen - WP)
                    return nvp_d[b, p0:p0 + r, :] if p0 >= 0 else None
                kvo = (kdst, vdst)
            descs.append(dict(s=b, ntok=T, c0=n * 4, first=(n == 0), y=yp_d[b, tok0:tok0 + T, :], x=xp[b, tok0:tok0 + T, :],
                              kv=kvo, u=(ncp_d[b, :, :] if n == NT - 1 else None),
                              pre_mid=((lambda b=b: load_gg(b)) if n == 0 else None)))
    gens = [tile(d["s"], k % 2, d["ntok"], d["c0"], d["first"], d["y"], kv_out=d["kv"], u_out=d["u"], pre_mid=d["pre_mid"])
            for k, d in enumerate(descs)]

    def run_until(g, label):
        while next(g) != label:
            pass

    issue_x_load(descs[0]["x"], descs[0]["ntok"], 0)
    if len(descs) > 1:
        issue_x_load(descs[1]["x"], descs[1]["ntok"], 1)
    run_until(gens[0], "head_done")
    for k in range(len(descs)):
        nxt = gens[k + 1] if k + 1 < len(descs) else None
        run_until(gens[k], "mid_done")
        for tail_lbl, head_lbl in (("t1", "h1"), ("t2", "h2"), ("t3", "h3"), ("t4", "head_done")):
            run_until(gens[k], tail_lbl)
            if nxt is not None:
                run_until(nxt, head_lbl)
        for _ in gens[k]:
            pass
        if k + 2 < len(descs):
            issue_x_load(descs[k + 2]["x"], descs[k + 2]["ntok"], k % 2)

    P.barrier()
    if os.environ.get("PHASE_DUMP"):
        import json
        json.dump(P.phases, open(os.environ["PHASE_DUMP"], "w"))
    P.emit_all(E)
    es.close()
    return nc


def _bias_gather_index():
    k = np.arange(128)[:, None]
    col = np.arange(256)[None, :]
    return np.clip(col - k, -128, 128) + 128


def make_core_inputs(inp, core, nseq, seqlen, shared):
    c_rows = [inp["c_prompt"][core * nseq + b] for b in range(nseq)] + [inp["c_sample"][core]]
    cmat = np.stack(c_rows, axis=0)
    cT = np.ascontiguousarray(cmat.T.reshape(8, 128, -1).transpose(1, 0, 2))
    m = {
        "xp": np.ascontiguousarray(inp["x_prompt"][core * nseq:(core + 1) * nseq]),
        "xs": np.ascontiguousarray(inp["x_sample"][core]),
        "cT": cT,
        "cconv": np.ascontiguousarray(inp["cache_conv"][0, core]),
        "ck": np.ascontiguousarray(inp["cache_k"][0, core].reshape(512, 512)),
        "cv": np.ascontiguousarray(inp["cache_v"][0, core].reshape(512, 512)),
    }
    m.update(shared)
    return m


def make_shared(inp):
    def cols(v, n):
        return np.asarray(v, np.float32).reshape(n, 128).T
    b_mod = np.asarray(inp["b_mod"][0], np.float32)
    dw_w = np.asarray(inp["dw_w"][0], np.float32)
    dww = dw_w.reshape(31, 4, 128).transpose(2, 1, 0).reshape(128, 124)
    vecs = np.concatenate([cols(inp["g_pre"][0], 8), cols(b_mod[0:D], 8), cols(b_mod[D:2 * D], 8), cols(inp["dw_b"][0], 4),
                           cols(inp["ln_g"][0], 4), cols(inp["ln_b"][0], 4), dww], axis=1).astype(np.float32)
    rb = np.asarray(inp["rel_bias"][0], np.float32)
    bt = np.ascontiguousarray(rb[:, _bias_gather_index()].transpose(1, 0, 2))
    ch = np.ascontiguousarray(np.broadcast_to(rb[:, 256][None, :], (128, 8)))
    return {
        "vecs": np.ascontiguousarray(vecs), "gpost": np.asarray(inp["g_post"], np.float32).reshape(1, D),
        "bmod": b_mod.reshape(1, 3 * D), "wmod": np.ascontiguousarray(inp["w_mod"][0]), "win": np.ascontiguousarray(inp["w_in"][0]),
        "wco": np.ascontiguousarray(inp["w_conv_out"][0]), "wao": np.ascontiguousarray(inp["w_attn_out"][0]),
        "wo": np.ascontiguousarray(inp["w_o"][0]), "bt": bt, "ch": ch, "ident": np.eye(128, dtype=np.float32),
    }


def run(inp, ncores, nseq, seqlen, stop=99, tstop=99):
    inp = {k: np.asarray(v) for k, v in inp.items()}
    nc = build_program(nseq, seqlen, stop, tstop)
    shared = make_shared(inp)
    in_maps = [make_core_inputs(inp, c, nseq, seqlen, shared) for c in range(ncores)]
    res = run_bass_kernel_spmd(nc, in_maps, core_ids=list(range(ncores)))
    R = res.results
    WP = min(512, seqlen)
    cat = lambda k: np.concatenate([r[k] for r in R], axis=0)
    stk = lambda k: np.stack([r[k] for r in R], axis=0)
    yp = cat("yp"); ys = stk("ys")
    ncp = cat("ncp")[None]; nkp = cat("nkp").reshape(1, ncores * nseq, WP, 8, 64); nvp = cat("nvp").reshape(1, ncores * nseq, WP, 8, 64)
    ncs = stk("ncs")[None]; nks = stk("nks").reshape(1, ncores, 512, 8, 64); nvs = stk("nvs").reshape(1, ncores, 512, 8, 64)
    return tuple(np.ascontiguousarray(a, dtype=np.float32) for a in (yp, ys, ncp, nkp, nvp, ncs, nks, nvs))


def kernel(**inputs):
    return run(inputs, 8, 2, 4096)
```

```python
import os
from contextlib import ExitStack
import numpy as np
import concourse.bass as bass
import concourse.mybir as mybir
from concourse.bass_utils import run_bass_kernel_spmd

F32 = mybir.dt.float32
BF16 = mybir.dt.bfloat16
AF = mybir.ActivationFunctionType
ALU = mybir.AluOpType

D = 1024
NIN = 5632
T = 256
DEC_T = 32
PAST = 1024
EPS = 1e-6
ENGINES = ("pe", "act", "dve", "pool", "sp")
NEG = -32768.0


class Prog:
    def __init__(self, nc):
        self.nc = nc
        self.ops = {e: [] for e in ENGINES}
        self.nops = {e: 0 for e in ENGINES}
        self.last_w = {}
        self.readers = {}
        self.waited = {e: {} for e in ENGINES}
        self.dma_cnt = {}
        self.sem_names = set(ENGINES)
        self.phase = "setup"
        self.phases = {e: [] for e in ENGINES}

    def _deps(self, eng, reads, writes):
        need = {}

        def add(tok):
            if tok is None:
                return
            s, v, e = tok
            if need.get(s, 0) < v:
                need[s] = v
        for k in reads:
            add(self.last_w.get(k))
        for k in writes:
            tok = self.last_w.get(k)
            if tok is not None and (tok[2] != eng or eng != "pe"):
                add(tok)
            for r in self.readers.get(k, ()):
                if r[2] != eng or eng != "pe":
                    add(r)
        waits = []
        for s, v in need.items():
            if self.waited[eng].get(s, 0) >= v:
                continue
            self.waited[eng][s] = v
            waits.append((s, v))
        return waits

    def _commit(self, tok, reads, writes):
        for k in reads:
            if k not in writes:
                self.readers.setdefault(k, []).append(tok)
        for k in writes:
            self.last_w[k] = tok
            self.readers[k] = []

    def op(self, eng, emit, reads=(), writes=()):
        waits = self._deps(eng, reads, writes)
        self.nops[eng] += 1
        tok = (eng, self.nops[eng], eng)
        self.ops[eng].append((waits, emit, (eng, 1)))
        self.phases[eng].append(self.phase)
        self._commit(tok, reads, writes)

    def dma(self, queue, sem, emit, reads=(), writes=()):
        self.sem_names.add(sem)
        waits = self._deps(queue, reads, writes)
        self.dma_cnt[sem] = self.dma_cnt.get(sem, 0) + 16
        tok = (sem, self.dma_cnt[sem], None)
        self.ops[queue].append((waits, emit, (sem, 16)))
        self._commit(tok, reads, writes)

    def barrier(self, engines=ENGINES):
        allw = [(e, self.nops[e]) for e in ENGINES if self.nops[e] > 0]
        allw += [(s, v) for s, v in self.dma_cnt.items()]
        for e in engines:
            waits = []
            for s, v in allw:
                if self.waited[e].get(s, 0) >= v:
                    continue
                self.waited[e][s] = v
                waits.append((s, v))
            self.ops[e].append((waits, None, None))

    def emit_all(self, enter):
        nc = self.nc
        sems = {s: enter(nc.semaphore("sem_" + s)) for s in sorted(self.sem_names)}
        block = enter(nc.Block())
        handles = {"pe": "tensor", "act": "scalar", "dve": "vector", "pool": "gpsimd", "sp": "sync"}

        def make(engname):
            def body(eng):
                for waits, emit, inc in self.ops[engname]:
                    for s, v in waits:
                        eng.wait_ge(sems[s], v)
                    if emit is None:
                        continue
                    emit(eng).then_inc(sems[inc[0]], inc[1])
            return body
        for e in ENGINES:
            if self.ops[e]:
                getattr(block, handles[e])(make(e))


V_GPRE, V_BSH, V_BSC, V_DWB, V_LNG, V_LNB, V_DWW = 0, 8, 16, 24, 28, 32, 36
NV = 36 + 124


def build_program(nseq, seqlen, stop=99, tstop=99):
    NT = seqlen // T
    NS = nseq + 1
    WP = min(512, seqlen)
    nc = bass.Bass("TRN2", target_bir_lowering=False)

    def din(name, shape, dt=F32):
        return nc.dram_tensor(name, list(shape), dt, kind="ExternalInput").ap()

    def dout(name, shape, dt=F32):
        return nc.dram_tensor(name, list(shape), dt, kind="ExternalOutput").ap()

    xp = din("xp", [nseq, seqlen, D]); xs_d = din("xs", [DEC_T, D])
    cT_d = din("cT", [128, 8, NS])
    cconv_d = din("cconv", [30, 512]); ck_d = din("ck", [512, 512]); cv_d = din("cv", [512, 512])
    vecs_d = din("vecs", [128, NV]); gpost_d = din("gpost", [1, D]); bmod_d = din("bmod", [1, 3 * D])
    wmod_d = din("wmod", [D, 3 * D]); win_d = din("win", [D, NIN])
    wco_d = din("wco", [512, D]); wao_d = din("wao", [512, D]); wo_d = din("wo", [D, D])
    bt_d = din("bt", [128, 8, 256]); ch_d = din("ch", [128, 8]); id_d = din("ident", [128, 128])

    yp_d = dout("yp", [nseq, seqlen, D]); ys_d = dout("ys", [DEC_T, D])
    ncp_d = dout("ncp", [nseq, 30, 512]); nkp_d = dout("nkp", [nseq, WP, 512]); nvp_d = dout("nvp", [nseq, WP, 512])
    ncs_d = dout("ncs", [30, 512]); nks_d = dout("nks", [512, 512]); nvs_d = dout("nvs", [512, 512])
    wbf_d = nc.dram_tensor("wbf", [11, 128, 8 * 512], BF16, kind="Internal").ap()
    gate_d = nc.dram_tensor("gate_scr", [NS, D], F32, kind="Internal").ap()

    es = ExitStack()
    E = es.enter_context
    P = Prog(nc)

    def sb(name, shape, dt=F32):
        return E(nc.sbuf_tensor(name, list(shape), dt))

    xin = [sb(f"xin{i}", [128, 2, D]) for i in range(2)]
    sqj = sb("sqj", [128, D], BF16)
    xsb = [sb(f"xsb{i}", [128, D], BF16) for i in range(2)]
    hT = sb("hT", [128, 8, T], BF16)
    NW = 3
    wst = [sb(f"wst{i}", [128, 8, 512], BF16) for i in range(NW)]
    uring = sb("uring", [128, 4, 30 + T], BF16)
    sigb = sb("sigb", [128, 4, T])
    szc = sb("szc", [128, 4, T], BF16)
    qT = sb("qT", [128, 4, T], BF16)
    kring = sb("kring", [128, 4, 768], BF16)
    vaug = [sb(f"vaug{i}", [128, 8, 65], BF16) for i in range(6)]
    sza = sb("sza", [128, 4, T], BF16)
    sgc = sb("sgc", [128, 8, T], BF16)
    sga = sb("sga", [128, 8, T], BF16)
    PT = [[sb(f"PT{p}_{t}", [128, T], BF16) for t in range(6)] for p in range(2)]
    BThi = sb("BThi", [128, 8, 256], BF16)
    BTlo = sb("BTlo", [128, 8, 256], BF16)
    BT = wst[0].bitcast(F32)
    chc = sb("chc", [128, 8])
    ycv = sb("ycv", [128, 4, T])
    ybf = sb("ybf", [128, 4, T], BF16)
    ysq = sb("ysq", [128, 4, T], BF16)
    mean_sb = sb("mean_sb", [128, T]); m2 = sb("m2", [128, T]); lrstd = sb("lrstd", [128, T])
    tt4 = sb("tt4", [128, 4, T])
    t1 = [tt4[:, i, :] for i in range(2)]
    t2 = [tt4[:, 2 + i, :] for i in range(2)]
    cin = sb("cin", [128, 4, T], BF16)
    rc = sb("rc", [128, 8])
    onorm = [sb(f"onorm{i}", [128, 512], BF16) for i in range(2)]
    oT = sb("oT", [128, 4, T], BF16)
    merged = sb("merged", [128, 8, T], BF16)
    mt2 = [sb(f"mt{i}", [128, 2, T]) for i in range(2)]
    gg = sb("gg", [128, D])
    ytmp = [tt4[:, 2 * i:2 * i + 2, :].rearrange("p a t -> p (a t)") for i in range(2)]
    wco = sb("wco_sb", [128, 4, D], BF16); wao = sb("wao_sb", [128, 4, D], BF16); wo = sb("wo_sb", [128, 8, D], BF16)
    diag = sb("diag", [128, 4 * 31, 128], BF16)
    ident_f = sb("ident_f", [128, 128]); ident = sb("ident_b", [128, 128], BF16)
    ones_m = sb("ones_m", [128, 128], BF16)
    vecs = sb("vecs_sb", [128, NV])
    bmod3 = xin[0][0:NS, :, :].rearrange("p a d -> p (a d)")
    cT = sb("cT_sb", [128, 8, NS])
    modsb_g = tt4[0:NS, :, :].rearrange("p c t -> p (c t)")
    mod_sh = sigb[0:NS, :, :].rearrange("p c t -> p (c t)")
    mod_sc = ycv[0:NS, :, :].rearrange("p c t -> p (c t)")
    sel = sb("sel", [NS, NS, 128])
    gs = sb("gs", [128, NS, 8]); shf = sb("shf", [128, NS, 8])
    st8 = sb("st8", [128, 8])
    tht = sb("tht", [128, 2, T])
    mhalf = sb("mhalf", [128, 1])
    kstage = sb("kstage", [128, 512]); kstage_b = sb("kstage_b", [128, 512], BF16)
    ostage = [sb(f"ostage{i}", [128, 512]) for i in range(2)]
    wmst = [xin[1][:, i, :] for i in range(2)]

    def pt(name):
        return E(nc.psum_tensor(name, [128, 512], F32))
    ps_mm = [pt("ps_mm0"), pt("ps_mm1")]
    ps_cv = [pt("ps_cv0"), pt("ps_cv1")]
    ps_st = pt("ps_st"); ps_sc = pt("ps_sc")
    ps_pv = [pt("ps_pvA"), pt("ps_pvB")]
    ps_st_bf = ps_st.bitcast(BF16)
    ps_sc_bf = ps_sc.bitcast(BF16)
    ps_mm1_bf = ps_mm[1].bitcast(BF16)

    def ld(q, sem, out_ap, in_ap, wkeys, rkeys=()):
        P.dma(q, sem, lambda e: e.dma_start(out=out_ap, in_=in_ap), reads=rkeys, writes=wkeys)

    ld("sp", "ldc1", vecs[:], vecs_d[:, :], ["vecs"])
    ld("sp", "ldc2", ident_f[:], id_d[:, :], ["ident_f"])
    ld("sp", "ldc3", cT[:], cT_d[:, :, :], ["cT"])
    ld("sp", "ldc4", BT[:], bt_d[:, :, :], ["BT"])
    ld("sp", "ldc5", chc[:], ch_d[:, :], ["chc"])
    ld("sp", "ldc7", gg[0:NS, :], bmod_d[0:1, 2 * D:3 * D].partition_broadcast(NS), ["bgate3"])
    ld("sp", "ldc8", bmod3, bmod_d[0:1, 0:2 * D].partition_broadcast(NS), ["bmod3"])
    ld("pool", "ldw", wco[:], wco_d.rearrange("(k p) n -> p k n", p=128), ["wco"])
    ld("pool", "ldw", wao[:], wao_d.rearrange("(k p) n -> p k n", p=128), ["wao"])
    ld("pool", "ldw", wo[:], wo_d.rearrange("(k p) n -> p k n", p=128), ["wo"])
    for g in range(11):
        ld("pool", "ldw", wbf_d[g].rearrange("p (k n) -> p k n", k=8),
           win_d[:, g * 512:(g + 1) * 512].rearrange("(k p) n -> p k n", p=128), [("wbf", g)])

    P.op("dve", lambda e: e.tensor_copy(out=ident[:], in_=ident_f[:]), reads=["ident_f"], writes=["ident"])
    P.op("dve", lambda e: e.memset(ones_m[:], 1.0 / 512.0), writes=["ones_m"])
    P.op("dve", lambda e: e.memset(mhalf[:], -0.5), writes=["mhalf"])
    P.op("dve", lambda e: e.memset(sel[:], 0.0), writes=["sel"])
    for s in range(NS):
        pass
    for s in range(NS):
        P.op("dve", lambda e, s=s: e.tensor_copy(out=sel[:, s, :], in_=ident_f[0:NS, s:s + 1].broadcast_to([NS, 128])),
             reads=["ident_f", "sel"], writes=["sel"])
    for m in range(124):
        P.op("dve", lambda e, m=m: e.tensor_scalar(out=diag[:, m, :], in0=ident_f[:], scalar1=vecs[:, V_DWW + m:V_DWW + m + 1],
                                                   scalar2=None, op0=ALU.mult),
             reads=["vecs", "ident_f"], writes=[("diag", m)])
    for h in range(8):
        P.op("dve", lambda e, h=h: e.tensor_scalar(out=BT[:, h, :], in0=BT[:, h, :], scalar1=chc[:, h:h + 1], scalar2=None,
                                                   op0=ALU.subtract), reads=["BT", "chc"], writes=["BT"])
    P.op("dve", lambda e: e.memset(BT[64:128, :, 0:64], NEG), reads=["BT"], writes=["BT"])
    P.op("dve", lambda e: e.tensor_scalar(out=BT[:, :, :], in0=BT[:, :, :], scalar1=8.0, scalar2=None, op0=ALU.mult), reads=["BT"], writes=["BT"])
    P.op("dve", lambda e: e.tensor_copy(out=BThi[:], in_=BT[:, :, :]), reads=["BT"], writes=["BThi"])
    P.op("dve", lambda e: e.tensor_tensor(out=BTlo[:], in0=BT[:, :, :], in1=BThi[:], op=ALU.subtract), reads=["BT", "BThi"], writes=["BTlo"])
    for p_ in range(2):
        for t in range(6):
            P.op("pool", lambda e, p_=p_, t=t: e.memset(PT[p_][t][:], 0.0), writes=[("PT", p_, t)])
    for i in range(6):
        P.op("pool", lambda e, i=i: e.memset(vaug[i][:, :, 64:65], 1.0), writes=[("vaug1", i)])

    def finish():
        P.barrier()
        P.emit_all(E)
        es.close()
        return nc
    if stop == 1:
        return finish()
    banks6 = [ps_mm[0], ps_mm[1], ps_cv[0], ps_cv[1], ps_st, ps_sc]
    bkeys = [("mm", 0), ("mm", 1), ("cv", 0), ("cv", 1), ("st",), ("sc",)]
    bkeys_full = {0: [("mm", 0, 0), ("mm", 0, 1)], 1: [("mm", 1, 0), ("mm", 1, 1)], 2: [("cv", 0), ("cv", 1)],
                  3: [("cv", 2), ("cv", 3)], 4: [("st", 0), ("st", 1)], 5: [("sc", 0), ("sc", 1)]}
    for kc in range(8):
        for half in range(3):
            slot = (kc * 3 + half) % 2
            ld("sp", f"wm{slot}", wmst[slot][:], wmod_d[kc * 128:(kc + 1) * 128, half * 1024:(half + 1) * 1024], [("wmst", slot)])
            for j in range(2):
                cg = half * 2 + j
                P.op("pe", lambda e, kc=kc, cg=cg, j=j, slot=slot: e.matmul(
                    banks6[cg][0:NS, :], lhsT=cT[:, kc, :], rhs=wmst[slot][:, j * 512:(j + 1) * 512],
                    start=(kc == 0), stop=(kc == 7)), reads=["cT", ("wmst", slot)], writes=bkeys_full[cg])
    for cg in range(6):
        dst = (mod_sh, mod_sh, mod_sc, mod_sc, modsb_g, modsb_g)[cg]
        badd = (bmod3[:, cg * 512:(cg + 1) * 512] if cg < 4 else gg[0:NS, (cg - 4) * 512:(cg - 3) * 512])
        P.op("dve", lambda e, cg=cg, dst=dst, badd=badd: e.tensor_tensor(out=dst[:, (cg % 2) * 512:(cg % 2 + 1) * 512], in0=banks6[cg][0:NS, :],
                                                                       in1=badd, op=ALU.add),
             reads=bkeys_full[cg] + ["bmod3", "bgate3"], writes=[("modsb", cg)])
    for which in range(2):
        for fc in range(8):
            col = which * D + fc * 128
            msrc = (mod_sh, mod_sc)[which]
            P.op("pe", lambda e, which=which, fc=fc, msrc=msrc: e.transpose(
                out=ps_pv[0][:, (which * 8 + fc) * NS:(which * 8 + fc + 1) * NS], in_=msrc[:, fc * 128:(fc + 1) * 128],
                identity=ident_f[0:NS, 0:NS]), reads=[("modsb", col // 512), "ident_f"], writes=[("pv", 0)])
    for s in range(NS):
        P.op("dve", lambda e, s=s: e.tensor_copy(
            out=shf[:, s, :], in_=ps_pv[0][:, 0:8 * NS].rearrange("p (f s) -> p s f", s=NS)[:, s, :]),
            reads=[("pv", 0)], writes=["shf"])
        P.op("dve", lambda e, s=s: e.scalar_tensor_tensor(
            out=gs[:, s, :], in0=ps_pv[0][:, 8 * NS:16 * NS].rearrange("p (f s) -> p s f", s=NS)[:, s, :],
            scalar=1.0, in1=vecs[:, V_GPRE:V_GPRE + 8], op0=ALU.add, op1=ALU.mult),
            reads=[("pv", 0), "vecs"], writes=["gs"])
    ld("sp", "gst", gate_d[:, :], modsb_g, [], rkeys=[("modsb", 4), ("modsb", 5)])
    P.barrier()

    state = {"wslot": 0, "xslot": 0, "ost": 0, "tile": 0, "sbank": 0, "nb": 0, "tht": 0}
    dve_act = ["dve", "act"]

    def load_gg(s):
        tmpk = [("t1", 0), ("t1", 1), ("t2", 0), ("t2", 1)]
        tmp = tt4[:, :, :].rearrange("p c t -> p (c t)")
        ld("pool", "ggl", gg[:, :], gate_d[s:s + 1, :].partition_broadcast(128), [("gg", 0), ("gg", 1)])
        ld("pool", "ggl", tmp, gpost_d[0:1, :].partition_broadcast(128), tmpk)
        P.op("dve", lambda e: e.tensor_tensor(out=gg[:, :], in0=gg[:, :], in1=tmp, op=ALU.mult),
             reads=tmpk, writes=[("gg", 0), ("gg", 1)])

    def issue_x_load(x_ap, ntok, slot):
        if ntok == T:
            ld("pool", f"x{slot}", xin[slot][:, :, :], x_ap.rearrange("(t p) d -> p t d", p=128), [("xin", slot, 0), ("xin", slot, 1)])
        else:
            ld("pool", f"x{slot}", xin[slot][0:ntok, 0, :], x_ap, [("xin", slot, 0)])

    def tile(s, x_slot, ntok, c0, first, y_ap, kv_out=None, u_out=None, pre_mid=None):
        cur = ["prenorm"]
        TT = (ntok + 127) // 128
        rows = [min(128, ntok - tt * 128) for tt in range(TT)]
        nq = (ntok + 63) // 64
        cur[0] = "prenorm"; P.phase = "prenorm"
        for tt in range(TT):
            r = rows[tt]
            xt = xin[x_slot][0:r, tt, :]
            P.op("act", lambda e, r=r, xt=xt, tt=tt: e.activation(out=sqj[0:r, :], in_=xt, func=AF.Square, accum_out=st8[0:r, tt:tt + 1]),
                 reads=[("xin", x_slot, tt)], writes=["sqj", ("st8", tt)])
            P.op("dve", lambda e, r=r, tt=tt: e.tensor_scalar(out=st8[0:r, 2 + tt:3 + tt], in0=st8[0:r, tt:tt + 1], scalar1=1.0 / D, scalar2=EPS,
                                                              op0=ALU.mult, op1=ALU.add), reads=[("st8", tt)], writes=[("st8", 2 + tt)])
            P.op("pool", lambda e, r=r, tt=tt: e.tensor_tensor(out=st8[0:r, 6 + tt:7 + tt], in0=st8[0:r, 2 + tt:3 + tt], in1=mhalf[0:r, :], op=ALU.pow),
                 reads=[("st8", 2 + tt)], writes=[("st8", 6 + tt)])
            P.op("dve", lambda e, r=r, xt=xt, tt=tt: e.tensor_scalar(out=xsb[tt][0:r, :], in0=xt, scalar1=st8[0:r, 6 + tt:7 + tt], scalar2=None,
                                                                     op0=ALU.mult), reads=[("xin", x_slot, tt), ("st8", 6 + tt)], writes=[("xsb", tt)])
            xbf = (ps_st_bf, ps_mm1_bf)[tt % 2]
            xbk = [("st", 0), ("st", 1)] if tt % 2 == 0 else [("mm", 1, 0), ("mm", 1, 1)]
            for fc in range(8):
                P.op("pe", lambda e, r=r, tt=tt, fc=fc, xbf=xbf: e.transpose(out=xbf[:, fc * 128:fc * 128 + r], in_=xsb[tt][0:r, fc * 128:(fc + 1) * 128],
                                                                             identity=ident[0:r, 0:r]),
                     reads=[("xsb", tt)], writes=xbk)
            xv = xbf[:, :].rearrange("p (f t) -> p f t", f=8)[:, :, 0:r]
            hv = hT[:, :, tt * 128:tt * 128 + r]
            hk = [("hT", fc, tt) for fc in range(8)]
            P.op("dve", lambda e, r=r, xv=xv, hv=hv: e.tensor_tensor(out=hv, in0=xv, in1=gs[:, s, :].unsqueeze(2).broadcast_to([128, 8, r]), op=ALU.mult),
                 reads=xbk, writes=hk)
            P.op("dve", lambda e, r=r, hv=hv: e.tensor_tensor(out=hv, in0=hv, in1=shf[:, s, :].unsqueeze(2).broadcast_to([128, 8, r]), op=ALU.add),
                 reads=hk, writes=hk)
        hT_keys = [("hT", fc, tt) for fc in range(8) for tt in range(TT)]

        yield "h1"

        P.phase = cur[0]
        cur[0] = "inproj"; P.phase = "inproj"
        mmslot = [0]

        def load_group(g):
            ws = state["wslot"]; state["wslot"] = (ws + 1) % NW
            ld("sp", f"w{ws}", wst[ws][:].rearrange("p k n -> p (k n)"), wbf_d[g], [("wst", ws)], rkeys=[("wbf", g)])
            return ws

        def fm_pair(ws, p, evac):
            sl = mmslot[0]; mmslot[0] = (sl + 1) % 2
            pk = [("mm", sl, 0), ("mm", sl, 1)]
            for half in range(2):
                c = 2 * p + half
                pa = ps_mm[sl][:, half * 256:half * 256 + ntok]
                for kc in range(8):
                    P.op("pe", lambda e, pa=pa, ws=ws, kc=kc, c=c: e.matmul(pa, lhsT=wst[ws][:, kc, c * 128:(c + 1) * 128], rhs=hT[:, kc, 0:ntok],
                                                                            start=(kc == 0), stop=(kc == 7)),
                         reads=[("wst", ws)] + hT_keys, writes=pk)
            bv = ps_mm[sl][:, :].rearrange("p (h t) -> p h t", h=2)[:, :, 0:ntok]
            evac(2 * p, bv, pk)

        def fm_group(g, evac):
            ws = load_group(g)
            for p in range(2):
                fm_pair(ws, p, evac)
            return ws

        def tm_group(g, evac, ws=None):
            if ws is None:
                ws = load_group(g)
            for tt in range(TT):
                r = rows[tt]
                bk = tt % 2
                pa = ps_mm[bk][0:r, :]
                pk = [("mm", bk, 0), ("mm", bk, 1)]
                for kc in range(8):
                    P.op("pe", lambda e, pa=pa, ws=ws, kc=kc, tt=tt, r=r: e.matmul(pa, lhsT=hT[:, kc, tt * 128:tt * 128 + r], rhs=wst[ws][:, kc, :],
                                                                                   start=(kc == 0), stop=(kc == 7)),
                         reads=[("wst", ws)] + hT_keys, writes=pk)
                evac(tt, r, pa, pk)
            mmslot[0] = 0
            return ws

        def out_stage(src_evac, dst_ap, r0, r1):
            so = state["ost"]; state["ost"] = (so + 1) % 2
            src_evac(ostage[so], ("ostage", so))
            ld("pool", f"os{so}", dst_ap, ostage[so][r0:r1, :], [], rkeys=[("ostage", so)])

        def silu2_fm(dst, key):
            def ev(c, pa, pk):
                P.op("act", lambda e: e.activation(out=tht[:, :, 0:ntok], in_=pa, func=AF.Tanh, scale=0.5), reads=pk, writes=["tht"])
                P.op("dve", lambda e: e.scalar_tensor_tensor(out=dst[:, c:c + 2, 0:ntok], in0=tht[:, :, 0:ntok], scalar=1.0, in1=pa, op0=ALU.add, op1=ALU.mult),
                     reads=pk + ["tht"], writes=[(key, c), (key, c + 1)])
            return ev

        def ev_b(c, pa, pk):
            P.op("act", lambda e: e.activation(out=sigb[:, c:c + 2, 0:ntok], in_=pa, func=AF.Tanh, scale=0.5), reads=pk, writes=[("sigb", c), ("sigb", c + 1)])
            P.op("dve", lambda e: e.tensor_scalar(out=sigb[:, c:c + 2, 0:ntok], in0=sigb[:, c:c + 2, 0:ntok], scalar1=0.5, scalar2=0.5, op0=ALU.mult, op1=ALU.add),
                 reads=[("sigb", c), ("sigb", c + 1)], writes=[("sigb", c), ("sigb", c + 1)])
        fm_group(1, ev_b)
        yield "h2"
        P.phase = cur[0]
        if first:
            P.op("dve", lambda e: e.memset(uring[:, :, 0:30], 0.0), writes=["uhist"])

        def ev_a(c, pa, pk):
            P.op("dve", lambda e: e.tensor_tensor(out=uring[:, c:c + 2, 30:30 + ntok], in0=pa, in1=sigb[:, c:c + 2, 0:ntok], op=ALU.mult),
                 reads=pk + [("sigb", c), ("sigb", c + 1)], writes=[("u", c), ("u", c + 1)])
        fm_group(0, ev_a)

        if u_out is not None:
            cur[0] = "uout"; P.phase = "uout"
            ttu = TT - 1; ru = rows[ttu]
            pk0 = [("mm", 0, 0), ("mm", 0, 1)]; pk1 = [("mm", 1, 0), ("mm", 1, 1)]
            pbu = ps_mm[0][0:ru, :]; pau = ps_mm[1][0:ru, :]
            wsb = load_group(1)
            for kc in range(8):
                P.op("pe", lambda e, kc=kc, pbu=pbu, ttu=ttu, ru=ru, wsb=wsb: e.matmul(
                    pbu, lhsT=hT[:, kc, ttu * 128:ttu * 128 + ru], rhs=wst[wsb][:, kc, :], start=(kc == 0), stop=(kc == 7)),
                    reads=[("wst", wsb)] + hT_keys, writes=pk0)
            P.op("act", lambda e, pbu=pbu, ru=ru: e.activation(out=kstage[0:ru, :], in_=pbu, func=AF.Tanh, scale=0.5), reads=pk0, writes=["kstage"])
            wsa = load_group(0)
            for kc in range(8):
                P.op("pe", lambda e, kc=kc, pau=pau, ttu=ttu, ru=ru, wsa=wsa: e.matmul(
                    pau, lhsT=hT[:, kc, ttu * 128:ttu * 128 + ru], rhs=wst[wsa][:, kc, :], start=(kc == 0), stop=(kc == 7)),
                    reads=[("wst", wsa)] + hT_keys, writes=pk1)

            def ev_u(o, ok, pau=pau, ru=ru):
                P.op("dve", lambda e: e.scalar_tensor_tensor(out=o[0:ru, :], in0=kstage[0:ru, :], scalar=1.0, in1=pau, op0=ALU.add, op1=ALU.mult),
                     reads=pk1 + ["kstage"], writes=[ok])
                P.op("dve", lambda e: e.tensor_scalar(out=o[0:ru, :], in0=o[0:ru, :], scalar1=0.5, scalar2=None, op0=ALU.mult), reads=[ok], writes=[ok])
            out_stage(ev_u, u_out, ru - 30, ru)
            mmslot[0] = 0
            cur[0] = "inproj"; P.phase = "inproj"

        def conv_chunk(c):
            ph = P.phase; P.phase = "conv"
            pc = (ps_sc, ps_st)[c % 2][:, 0:ntok]
            cvk = [(("sc", "st")[c % 2], 0), (("sc", "st")[c % 2], 1)]
            for k in range(31):
                P.op("pe", lambda e, pc=pc, c=c, k=k: e.matmul(pc, lhsT=diag[:, c * 31 + k, :], rhs=uring[:, c, k:k + ntok], start=(k == 0), stop=(k == 30)),
                     reads=[("u", c), "uhist"], writes=cvk)
            P.op("dve", lambda e, pc=pc, c=c: e.tensor_scalar(out=ycv[:, c, 0:ntok], in0=pc, scalar1=vecs[:, V_DWB + c:V_DWB + c + 1], scalar2=None, op0=ALU.add),
                 reads=cvk, writes=[("ycv", c)])
            P.op("act", lambda e, c=c: e.activation(out=ysq[:, c, 0:ntok], in_=ycv[:, c, 0:ntok], func=AF.Square),
                 reads=[("ycv", c)], writes=[("ysq", c)])
            P.op("dve", lambda e, c=c: e.tensor_copy(out=ybf[:, c, 0:ntok], in_=ycv[:, c, 0:ntok]), reads=[("ycv", c)], writes=[("ybf", c)])
            P.phase = ph

        yield "h3"

        P.phase = cur[0]
        fm_group(2, silu2_fm(szc, "szc"))
        yield "head_done"
        P.phase = cur[0]
        if pre_mid is not None:
            pre_mid()
        conv_chunk(0)

        def ev_q(c, pa, pk):
            P.op("dve", lambda e: e.tensor_copy(out=qT[:, c:c + 2, 0:ntok], in_=pa), reads=pk, writes=[("qT", c), ("qT", c + 1)])
        fm_group(3, ev_q)
        conv_chunk(1)

        kcol = (c0 % 12) * 64

        def ev_k(c, pa, pk):
            P.op("act", lambda e: e.activation(out=kring[:, c:c + 2, kcol:kcol + ntok], in_=pa, func=AF.Copy), reads=pk,
                 writes=[("k", cc, (c0 % 12) // 2 + j) for cc in (c, c + 1) for j in range((ntok + 127) // 128)])
        ws_k = fm_group(4, ev_k)
        if kv_out is not None:
            def ev_ktm(tt, r, pa, pk):
                dst = kv_out[0](tt, r)
                if dst is None:
                    return
                out_stage(lambda o, ok: P.op("act", lambda e: e.activation(out=o[0:r, :], in_=pa, func=AF.Copy), reads=pk, writes=[ok]), dst, 0, r)
            tm_group(4, ev_ktm, ws=ws_k)
        conv_chunk(2)

        a0 = c0 // 2

        def ev_v(tt, r, pa, pk):
            vs = (a0 + tt) % 6
            dst = kv_out[1](tt, r) if kv_out is not None else None
            if dst is None:
                P.op("dve", lambda e: e.tensor_copy(out=vaug[vs][0:r, :, 0:64], in_=pa.rearrange("p (h d) -> p h d", d=64)),
                     reads=pk, writes=[("v", vs)])
            else:
                so = state["ost"]; state["ost"] = (so + 1) % 2
                P.op("act", lambda e: e.activation(out=ostage[so][0:r, :], in_=pa, func=AF.Copy), reads=pk, writes=[("ostage", so)])
                P.op("dve", lambda e: e.tensor_copy(out=vaug[vs][0:r, :, 0:64], in_=ostage[so][0:r, :].rearrange("p (h d) -> p h d", d=64)),
                     reads=[("ostage", so)], writes=[("v", vs)])
                ld("pool", f"os{so}", dst, ostage[so][0:r, :], [], rkeys=[("ostage", so)])
        tm_group(5, ev_v)
        conv_chunk(3)

        cur[0] = "conv"; P.phase = "conv"
        P.op("dve", lambda e: e.tensor_copy(out=uring[:, :, 0:30], in_=uring[:, :, ntok:ntok + 30]),
             reads=[("u", c) for c in range(4)], writes=["uhist"])
        pmean = ps_st[:, 0:ntok]; pmsq = ps_st[:, 256:256 + ntok]
        for c in range(4):
            P.op("pe", lambda e, c=c: e.matmul(pmean, lhsT=ones_m[:], rhs=ybf[:, c, 0:ntok], start=(c == 0), stop=(c == 3)),
                 reads=[("ybf", c)], writes=[("st", 0), ("st", 1)])
        for c in range(4):
            P.op("pe", lambda e, c=c: e.matmul(pmsq, lhsT=ones_m[:], rhs=ysq[:, c, 0:ntok], start=(c == 0), stop=(c == 3)),
                 reads=[("ysq", c)], writes=[("st", 0), ("st", 1)])
        P.op("act", lambda e: e.activation(out=mean_sb[:, 0:ntok], in_=pmean, func=AF.Copy), reads=[("st", 0), ("st", 1)], writes=["mean_sb"])
        P.op("dve", lambda e: e.tensor_tensor(out=m2[:, 0:ntok], in0=mean_sb[:, 0:ntok], in1=mean_sb[:, 0:ntok], op=ALU.mult), reads=["mean_sb"], writes=["m2"])
        P.op("dve", lambda e: e.tensor_tensor(out=m2[:, 0:ntok], in0=pmsq, in1=m2[:, 0:ntok], op=ALU.subtract), reads=[("st", 0), ("st", 1), "m2"], writes=["m2"])
        P.op("dve", lambda e: e.tensor_scalar(out=m2[:, 0:ntok], in0=m2[:, 0:ntok], scalar1=0.0, scalar2=EPS, op0=ALU.max, op1=ALU.add), reads=["m2"], writes=["m2"])

        def ln_rstd():
            P.op("act", lambda e: e.activation(out=m2[:, 0:ntok], in_=m2[:, 0:ntok], func=AF.Sqrt), reads=["m2"], writes=["m2"])
            P.op("dve", lambda e: e.reciprocal(out=lrstd[:, 0:ntok], in_=m2[:, 0:ntok]), reads=["m2"], writes=["lrstd"])
        def ln_tail():
            cur[0] = "conv"; P.phase = "conv"
            allc = [("ycv", c) for c in range(4)]
            tk4 = [("t1", 0), ("t1", 1), ("t2", 0), ("t2", 1)]
            mean_b = mean_sb[:, 0:ntok].unsqueeze(1).broadcast_to([128, 4, ntok])
            rstd_b = lrstd[:, 0:ntok].unsqueeze(1).broadcast_to([128, 4, ntok])
            P.op("dve", lambda e: e.tensor_tensor(out=ycv[:, :, 0:ntok], in0=ycv[:, :, 0:ntok], in1=mean_b, op=ALU.subtract),
                 reads=allc + ["mean_sb"], writes=allc)
            P.op("dve", lambda e: e.tensor_tensor(out=ycv[:, :, 0:ntok], in0=ycv[:, :, 0:ntok], in1=rstd_b, op=ALU.mult),
                 reads=allc + ["lrstd"], writes=allc)
            for c in range(4):
                P.op("dve", lambda e, c=c: e.tensor_scalar(out=ycv[:, c, 0:ntok], in0=ycv[:, c, 0:ntok], scalar1=vecs[:, V_LNG + c:V_LNG + c + 1],
                                                           scalar2=vecs[:, V_LNB + c:V_LNB + c + 1], op0=ALU.mult, op1=ALU.add),
                     reads=[("ycv", c)], writes=[("ycv", c)])
            cur[0] = "attn"; P.phase = "attn"

        def ln_tail_b():
            cur[0] = "conv"; P.phase = "conv"
            allc = [("ycv", c) for c in range(4)]
            tk4 = [("t1", 0), ("t1", 1), ("t2", 0), ("t2", 1)]
            P.op("act", lambda e: e.activation(out=tt4[:, :, 0:ntok], in_=ycv[:, :, 0:ntok], func=AF.Tanh, scale=0.5), reads=allc, writes=tk4)
            P.op("dve", lambda e: e.scalar_tensor_tensor(out=tt4[:, :, 0:ntok], in0=tt4[:, :, 0:ntok], scalar=1.0, in1=ycv[:, :, 0:ntok], op0=ALU.add, op1=ALU.mult),
                 reads=allc + tk4, writes=tk4)
            P.op("dve", lambda e: e.scalar_tensor_tensor(out=cin[:, :, 0:ntok], in0=tt4[:, :, 0:ntok], scalar=0.25, in1=szc[:, :, 0:ntok], op0=ALU.mult, op1=ALU.mult),
                 reads=tk4 + [("szc", c) for c in range(4)], writes=[("cin", c) for c in range(4)])
            cur[0] = "attn"; P.phase = "attn"

        def ev_gate(dstg, keyg, off):
            def ev(c, pa, pk):
                P.op("act", lambda e: e.activation(out=dstg[:, off + c:off + c + 2, 0:ntok], in_=pa, func=AF.Tanh, scale=0.5), reads=pk,
                     writes=[(keyg, off + c), (keyg, off + c + 1)])
            return ev
        late = {0: [(6, silu2_fm(sza, "sza"))], 1: [(7, ev_gate(sgc, "sgc", 0))], 2: [(8, ev_gate(sgc, "sgc", 4))],
                3: [(9, ev_gate(sga, "sga", 0))], 4: [(10, ev_gate(sga, "sga", 4))]}

        cur[0] = "attn"; P.phase = "attn"
        kcur_end = c0 * 64 + ntok
        tiles = []
        for t in range(6):
            a = a0 - 4 + t
            if a < 0:
                continue
            nk = min(128, kcur_end - a * 128)
            if nk <= 0:
                continue
            tiles.append((t, a, nk))
        QB = (ntok + 127) // 128
        s_banks = [(ps_sc, "sc", 0), (ps_st, "st", 0), (ps_cv[0], "cv", 0), (ps_cv[1], "cv", 2)]
        for h in range(8):
            if h == 6:
                yield "mid_done"
                P.phase = cur[0]
            if h == 7:
                yield "t1"
                P.phase = cur[0]
            j, hp = h // 2, h % 2
            par = h % 2
            prow = slice(hp * 64, hp * 64 + 64)
            for (t, a, nk) in tiles:
                i_lo = max(0, 2 * t - 8); i_hi = min(nq - 1, 2 * t + 1)
                if i_lo > i_hi:
                    continue
                c_lo = i_lo * 64; c_hi = min((i_hi + 1) * 64, ntok)
                sbk = s_banks[state["sbank"] % 4]; state["sbank"] += 1
                psS = sbk[0][:, 0:256]
                sck = [(sbk[1], sbk[2]), (sbk[1], sbk[2] + 1)]
                kph = (a % 6) * 128
                near_t = max(0, 2 * t - 8) <= min(nq - 1, 2 * t - 5)
                P.op("pe", lambda e, psS=psS, nk=nk, c_lo=c_lo, c_hi=c_hi, j=j, prow=prow, kph=kph, near_t=near_t: e.matmul(
                    psS[0:nk, c_lo:c_hi], lhsT=kring[prow, j, kph:kph + nk], rhs=qT[prow, j, c_lo:c_hi], start=True, stop=(not near_t)),
                    reads=[("k", j, a % 6), ("qT", j)], writes=sck)
                ptile = PT[par][t]
                pkey = ("PT", par, t)
                n_lo = max(0, 2 * t - 8); n_hi = min(nq - 1, 2 * t - 5)
                has_near = n_lo <= n_hi
                if has_near:
                    q0 = n_lo * 64; q1 = min((n_hi + 1) * 64, ntok); w = q1 - q0
                    b0 = (8 + n_lo - 2 * t) * 64
                    for bi, btile in enumerate((BThi, BTlo)):
                        P.op("pe", lambda e, psS=psS, nk=nk, q0=q0, q1=q1, w=w, b0=b0, h=h, btile=btile, bi=bi: e.matmul(
                            psS[0:nk, q0:q1], lhsT=ident[0:nk, 0:nk], rhs=btile[0:nk, h, b0:b0 + w], start=False, stop=(bi == 1)),
                            reads=[], writes=sck)
                f_lo = max(0, 2 * t - 4); f_hi = min(nq - 1, 2 * t)
                has_far = f_lo <= f_hi
                if has_near or has_far:
                    e_lo = n_lo if has_near else f_lo
                    e_hi = f_hi if has_far else n_hi
                    q0 = e_lo * 64; q1 = min((e_hi + 1) * 64, ntok)
                    P.op("act", lambda e, ptile=ptile, psS=psS, nk=nk, q0=q0, q1=q1: e.activation(
                        out=ptile[0:nk, q0:q1], in_=psS[0:nk, q0:q1], func=AF.Exp, scale=0.125), reads=sck, writes=[pkey])
                ie = 2 * t + 1
                if ie <= nq - 1 and ie >= max(0, 2 * t - 4) and nk > 64:
                    q0 = ie * 64; q1 = min((ie + 1) * 64, ntok)
                    P.op("act", lambda e, ptile=ptile, psS=psS, nk=nk, q0=q0, q1=q1: e.activation(
                        out=ptile[64:nk, q0:q1], in_=psS[64:nk, q0:q1], func=AF.Exp, scale=0.125), reads=sck, writes=[pkey])
            if h == 0:
                ln_rstd()
            for (g, ev) in late.get(h, ()):
                cur[0] = "inproj"; P.phase = "inproj"
                fm_group(g, ev)
                cur[0] = "attn"; P.phase = "attn"
            for qb in range(QB):
                r = rows[qb]
                bank = ps_pv[qb]
                hh = h % 4
                use = [(t, a, nk) for (t, a, nk) in tiles if qb <= t <= qb + 4]
                for idx, (t, a, nk) in enumerate(use):
                    P.op("pe", lambda e, bank=bank, hh=hh, r=r, qb=qb, t=t, a=a, nk=nk, idx=idx, nuse=len(use), par=par, h=h: e.matmul(
                        bank[0:r, hh * 65:hh * 65 + 65],
                        lhsT=PT[par][t][0:nk, qb * 128:qb * 128 + r], rhs=vaug[a % 6][0:nk, h, :], start=(idx == 0), stop=(idx == nuse - 1)),
                        reads=[("PT", par, t), ("v", a % 6), ("vaug1", a % 6)], writes=[("pv", qb)])
                P.op("dve", lambda e, bank=bank, hh=hh, r=r, h=h: e.reciprocal(out=rc[0:r, h:h + 1], in_=bank[0:r, hh * 65 + 64:hh * 65 + 65]),
                     reads=[("pv", qb)], writes=[("rc", h)])
                P.op("dve", lambda e, bank=bank, hh=hh, r=r, h=h, qb=qb: e.tensor_scalar(
                    out=onorm[qb][0:r, h * 64:(h + 1) * 64], in0=bank[0:r, hh * 65:hh * 65 + 64], scalar1=rc[0:r, h:h + 1], scalar2=None, op0=ALU.mult),
                    reads=[("pv", qb), ("rc", h)], writes=[("onorm", qb, h)])
            if h == 0:
                ln_tail()
            if h == 1:
                ln_tail_b()
        yield "t2"
        P.phase = cur[0]
        for qb in range(QB):
            r = rows[qb]
            obf = (ps_st_bf, ps_sc_bf)[qb % 2]
            obk = [(("st", "sc")[qb % 2], 0), (("st", "sc")[qb % 2], 1)]
            for fc in range(4):
                P.op("pe", lambda e, qb=qb, r=r, fc=fc, obf=obf: e.transpose(out=obf[:, fc * 128:fc * 128 + r], in_=onorm[qb][0:r, fc * 128:(fc + 1) * 128],
                                                                             identity=ident[0:r, 0:r]),
                     reads=[("onorm", qb, 2 * fc), ("onorm", qb, 2 * fc + 1)], writes=obk)
            for fc in range(4):
                P.op("dve", lambda e, qb=qb, r=r, fc=fc, obf=obf: e.scalar_tensor_tensor(out=oT[:, fc, qb * 128:qb * 128 + r], in0=obf[:, fc * 128:fc * 128 + r],
                                                                                         scalar=0.5, in1=sza[:, fc, qb * 128:qb * 128 + r], op0=ALU.mult, op1=ALU.mult),
                     reads=obk + [("sza", fc)], writes=[("oT", fc, qb)])

        yield "t3"

        P.phase = cur[0]
        cur[0] = "outproj"; P.phase = "outproj"
        for fp in range(4):
            fo0 = 2 * fp
            bka = (ps_mm[0], ps_mm[1], ps_cv[0], ps_cv[1])[fp]
            pak = ([("mm", 0, 0), ("mm", 0, 1)], [("mm", 1, 0), ("mm", 1, 1)], [("cv", 0), ("cv", 1)], [("cv", 2), ("cv", 3)])[fp]
            bkb = (ps_sc, ps_st, ps_pv[0], ps_pv[1])[fp]
            pbk = ([("sc", 0), ("sc", 1)], [("st", 0), ("st", 1)], [("pv", 0)], [("pv", 1)])[fp]
            for half in range(2):
                fo = fo0 + half
                pa = bka[:, half * 256:half * 256 + ntok]
                pb = bkb[:, half * 256:half * 256 + ntok]
                for kc in range(4):
                    P.op("pe", lambda e, pa=pa, kc=kc, fo=fo: e.matmul(pa, lhsT=wco[:, kc, fo * 128:(fo + 1) * 128], rhs=cin[:, kc, 0:ntok], start=(kc == 0), stop=(kc == 3)),
                         reads=[("cin", kc)], writes=pak)
                for kc in range(4):
                    P.op("pe", lambda e, pb=pb, kc=kc, fo=fo: e.matmul(pb, lhsT=wao[:, kc, fo * 128:(fo + 1) * 128], rhs=oT[:, kc, 0:ntok], start=(kc == 0), stop=(kc == 3)),
                         reads=[("oT", kc, qb) for qb in range(QB)], writes=pbk)
            va = bka[:, :].rearrange("p (h t) -> p h t", h=2)[:, :, 0:ntok]
            vb = bkb[:, :].rearrange("p (h t) -> p h t", h=2)[:, :, 0:ntok]
            P.op("dve", lambda e, va=va, fo0=fo0: e.scalar_tensor_tensor(out=mt2[0][:, :, 0:ntok], in0=sgc[:, fo0:fo0 + 2, 0:ntok], scalar=1.0, in1=va, op0=ALU.add, op1=ALU.mult),
                 reads=pak + [("sgc", fo0), ("sgc", fo0 + 1)], writes=[("mt", 0)])
            P.op("dve", lambda e, vb=vb, fo0=fo0: e.scalar_tensor_tensor(out=mt2[1][:, :, 0:ntok], in0=sga[:, fo0:fo0 + 2, 0:ntok], scalar=1.0, in1=vb, op0=ALU.add, op1=ALU.mult),
                 reads=pbk + [("sga", fo0), ("sga", fo0 + 1)], writes=[("mt", 1)])
            P.op("dve", lambda e, fo0=fo0: e.tensor_tensor(out=merged[:, fo0:fo0 + 2, 0:ntok], in0=mt2[0][:, :, 0:ntok], in1=mt2[1][:, :, 0:ntok], op=ALU.add),
                 reads=[("mt", 0), ("mt", 1)], writes=[("merged", fo0), ("merged", fo0 + 1)])

        yield "t4"

        P.phase = cur[0]
        cur[0] = "wo"; P.phase = "wo"
        def wok(tt, hf):
            return [("cv", 2 * hf), ("cv", 2 * hf + 1)] if tt % 2 == 0 else [("mm", hf, 0), ("mm", hf, 1)]
        for tt in range(TT):
            r = rows[tt]
            for hf in range(2):
                po = (ps_cv, ps_mm)[tt % 2][hf][0:r, :]
                for kc in range(8):
                    P.op("pe", lambda e, po=po, kc=kc, tt=tt, r=r, hf=hf: e.matmul(po, lhsT=merged[:, kc, tt * 128:tt * 128 + r], rhs=wo[:, kc, hf * 512:(hf + 1) * 512],
                                                                                   start=(kc == 0), stop=(kc == 7)),
                         reads=[("merged", kc)], writes=wok(tt, hf))
                P.op("act", lambda e, po=po, r=r, hf=hf: e.activation(out=sqj[0:r, 0:512], in_=po, func=AF.Square, accum_out=st8[0:r, hf:hf + 1]),
                     reads=wok(tt, hf), writes=["sqj", ("st8", hf)])
            P.op("dve", lambda e, r=r: e.tensor_tensor(out=st8[0:r, 2:3], in0=st8[0:r, 0:1], in1=st8[0:r, 1:2], op=ALU.add),
                 reads=[("st8", 0), ("st8", 1)], writes=[("st8", 2)])
            P.op("dve", lambda e, r=r: e.tensor_scalar(out=st8[0:r, 3:4], in0=st8[0:r, 2:3], scalar1=1.0 / D, scalar2=4.0 * EPS, op0=ALU.mult, op1=ALU.add),
                 reads=[("st8", 2)], writes=[("st8", 3)])
            P.op("pool", lambda e, r=r: e.tensor_tensor(out=st8[0:r, 6:7], in0=st8[0:r, 3:4], in1=mhalf[0:r, :], op=ALU.pow), reads=[("st8", 3)], writes=[("st8", 6)])
            for hf in range(2):
                po = (ps_cv, ps_mm)[tt % 2][hf][0:r, :]
                P.op("dve", lambda e, po=po, r=r, hf=hf: e.scalar_tensor_tensor(out=ytmp[hf][0:r, :], in0=po, scalar=st8[0:r, 6:7], in1=gg[0:r, hf * 512:(hf + 1) * 512],
                                                                                op0=ALU.mult, op1=ALU.mult),
                     reads=wok(tt, hf) + [("st8", 6), ("gg", hf)], writes=[(("t1", "t2")[hf], 0), (("t1", "t2")[hf], 1)])
                P.op("dve", lambda e, r=r, hf=hf, tt=tt: e.tensor_tensor(out=xin[x_slot][0:r, tt, hf * 512:(hf + 1) * 512], in0=xin[x_slot][0:r, tt, hf * 512:(hf + 1) * 512],
                                                                          in1=ytmp[hf][0:r, :], op=ALU.add),
                     reads=[(("t1", "t2")[hf], 0), (("t1", "t2")[hf], 1), ("xin", x_slot, tt)], writes=[("xin", x_slot, tt)])
        if ntok == T:
            ld("pool", f"yo{x_slot}", y_ap.rearrange("(t p) d -> p t d", p=128), xin[x_slot][:, :, :], [], rkeys=[("xin", x_slot, 0), ("xin", x_slot, 1)])
        else:
            ld("pool", f"yo{x_slot}", y_ap, xin[x_slot][0:ntok, 0, :], [], rkeys=[("xin", x_slot, 0)])

    s_samp = nseq
    issue_x_load(xs_d[:, :], DEC_T, 0)
    for i in range(4):
        a = 4 + i
        ld("sp", "ldk", kstage[:], ck_d[i * 128:(i + 1) * 128, :], ["kstage"])
        P.op("dve", lambda e: e.tensor_copy(out=kstage_b[:], in_=kstage[:]), reads=["kstage"], writes=["kstage_b"])
        for j in range(4):
            P.op("pe", lambda e, j=j: e.transpose(out=ps_st_bf[:, j * 128:(j + 1) * 128], in_=kstage_b[:, j * 128:(j + 1) * 128], identity=ident[:]),
                 reads=["kstage_b"], writes=[("st", 0), ("st", 1)])
        P.op("act", lambda e, a=a: e.activation(out=kring[:, :, (a % 6) * 128:(a % 6) * 128 + 128],
                                                in_=ps_st_bf[:, 0:512].rearrange("p (j t) -> p j t", j=4), func=AF.Copy),
             reads=[("st", 0), ("st", 1)], writes=[("k", j, a % 6) for j in range(4)])
        ld("sp", f"ldv{i}", ostage[i % 2][:], cv_d[i * 128:(i + 1) * 128, :], [("ostage", i % 2)])
        P.op("dve", lambda e, a=a, i=i: e.tensor_copy(out=vaug[a % 6][:, :, 0:64], in_=ostage[i % 2][:].rearrange("p (h d) -> p h d", d=64)),
             reads=[("ostage", i % 2)], writes=[("v", a % 6)])
    ld("sp", "ldk", kstage[0:30, :], cconv_d[:, :], ["kstage"])
    P.op("dve", lambda e: e.tensor_copy(out=kstage_b[0:30, :], in_=kstage[0:30, :]), reads=["kstage"], writes=["kstage_b"])
    for c in range(4):
        P.op("pe", lambda e, c=c: e.transpose(out=ps_st_bf[:, c * 128:c * 128 + 30], in_=kstage_b[0:30, c * 128:(c + 1) * 128], identity=ident[0:30, 0:30]),
             reads=["kstage_b"], writes=[("st", 0), ("st", 1)])
    P.op("act", lambda e: e.activation(out=uring[:, :, 0:30], in_=ps_st_bf[:, 0:512].rearrange("p (c t) -> p c t", c=4)[:, :, 0:30], func=AF.Copy),
         reads=[("st", 0), ("st", 1)], writes=["uhist"])
    ld("sp", "cpy", nks_d[0:480, :], ck_d[32:512, :], [])
    ld("sp", "cpy", nvs_d[0:480, :], cv_d[32:512, :], [])
    descs = [dict(s=s_samp, ntok=DEC_T, c0=PAST // 64, first=False, y=ys_d[:, :], x=xs_d[:, :],
                  kv=(lambda tt, r: nks_d[480:512, :], lambda tt, r: nvs_d[480:512, :]), u=ncs_d[:, :],
                  pre_mid=(lambda: load_gg(s_samp)))]
    for b in range(nseq):
        for n in range(NT):
            tok0 = n * T
            kvo = None
            if tok0 + T > seqlen - WP:
                def kdst(tt, r, b=b, tok0=tok0):
                    p0 = tok0 + tt * 128 - (seqlen - WP)
                    return nkp_d[b, p0:p0 + r, :] if p0 >= 0 else None

                def vdst(tt, r, b=b, tok0=tok0):
                    p0 = tok0 + tt * 128 - (seqlen - WP)
                    return nvp_d[b, p0:p0 + r, :] if p0 >= 0 else None
                kvo = (kdst, vdst)
            descs.append(dict(s=b, ntok=T, c0=n * 4, first=(n == 0), y=yp_d[b, tok0:tok0 + T, :], x=xp[b, tok0:tok0 + T, :],
                              kv=kvo, u=(ncp_d[b, :, :] if n == NT - 1 else None),
                              pre_mid=((lambda b=b: load_gg(b)) if n == 0 else None)))
    gens = [tile(d["s"], k % 2, d["ntok"], d["c0"], d["first"], d["y"], kv_out=d["kv"], u_out=d["u"], pre_mid=d["pre_mid"])
            for k, d in enumerate(descs)]

    def run_until(g, label):
        while next(g) != label:
            pass

    issue_x_load(descs[0]["x"], descs[0]["ntok"], 0)
    if len(descs) > 1:
        issue_x_load(descs[1]["x"], descs[1]["ntok"], 1)
    run_until(gens[0], "head_done")
    for k in range(len(descs)):
        nxt = gens[k + 1] if k + 1 < len(descs) else None
        run_until(gens[k], "mid_done")
        for tail_lbl, head_lbl in (("t1", "h1"), ("t2", "h2"), ("t3", "h3"), ("t4", "head_done")):
            run_until(gens[k], tail_lbl)
            if nxt is not None:
                run_until(nxt, head_lbl)
        for _ in gens[k]:
            pass
        if k + 2 < len(descs):
            issue_x_load(descs[k + 2]["x"], descs[k + 2]["ntok"], k % 2)

    P.barrier()
    if os.environ.get("PHASE_DUMP"):
        import json
        json.dump(P.phases, open(os.environ["PHASE_DUMP"], "w"))
    P.emit_all(E)
    es.close()
    return nc


def _bias_gather_index():
    k = np.arange(128)[:, None]
    col = np.arange(256)[None, :]
    return np.clip(col - k, -128, 128) + 128


def make_core_inputs(inp, core, nseq, seqlen, shared):
    c_rows = [inp["c_prompt"][core * nseq + b] for b in range(nseq)] + [inp["c_sample"][core]]
    cmat = np.stack(c_rows, axis=0)
    cT = np.ascontiguousarray(cmat.T.reshape(8, 128, -1).transpose(1, 0, 2))
    m = {
        "xp": np.ascontiguousarray(inp["x_prompt"][core * nseq:(core + 1) * nseq]),
        "xs": np.ascontiguousarray(inp["x_sample"][core]),
        "cT": cT,
        "cconv": np.ascontiguousarray(inp["cache_conv"][0, core]),
        "ck": np.ascontiguousarray(inp["cache_k"][0, core].reshape(512, 512)),
        "cv": np.ascontiguousarray(inp["cache_v"][0, core].reshape(512, 512)),
    }
    m.update(shared)
    return m


def make_shared(inp):
    def cols(v, n):
        return np.asarray(v, np.float32).reshape(n, 128).T
    b_mod = np.asarray(inp["b_mod"][0], np.float32)
    dw_w = np.asarray(inp["dw_w"][0], np.float32)
    dww = dw_w.reshape(31, 4, 128).transpose(2, 1, 0).reshape(128, 124)
    vecs = np.concatenate([cols(inp["g_pre"][0], 8), cols(b_mod[0:D], 8), cols(b_mod[D:2 * D], 8), cols(inp["dw_b"][0], 4),
                           cols(inp["ln_g"][0], 4), cols(inp["ln_b"][0], 4), dww], axis=1).astype(np.float32)
    rb = np.asarray(inp["rel_bias"][0], np.float32)
    bt = np.ascontiguousarray(rb[:, _bias_gather_index()].transpose(1, 0, 2))
    ch = np.ascontiguousarray(np.broadcast_to(rb[:, 256][None, :], (128, 8)))
    return {
        "vecs": np.ascontiguousarray(vecs), "gpost": np.asarray(inp["g_post"], np.float32).reshape(1, D),
        "bmod": b_mod.reshape(1, 3 * D), "wmod": np.ascontiguousarray(inp["w_mod"][0]), "win": np.ascontiguousarray(inp["w_in"][0]),
        "wco": np.ascontiguousarray(inp["w_conv_out"][0]), "wao": np.ascontiguousarray(inp["w_attn_out"][0]),
        "wo": np.ascontiguousarray(inp["w_o"][0]), "bt": bt, "ch": ch, "ident": np.eye(128, dtype=np.float32),
    }


def run(inp, ncores, nseq, seqlen, stop=99, tstop=99):
    inp = {k: np.asarray(v) for k, v in inp.items()}
    nc = build_program(nseq, seqlen, stop, tstop)
    shared = make_shared(inp)
    in_maps = [make_core_inputs(inp, c, nseq, seqlen, shared) for c in range(ncores)]
    res = run_bass_kernel_spmd(nc, in_maps, core_ids=list(range(ncores)))
    R = res.results
    WP = min(512, seqlen)
    cat = lambda k: np.concatenate([r[k] for r in R], axis=0)
    stk = lambda k: np.stack([r[k] for r in R], axis=0)
    yp = cat("yp"); ys = stk("ys")
    ncp = cat("ncp")[None]; nkp = cat("nkp").reshape(1, ncores * nseq, WP, 8, 64); nvp = cat("nvp").reshape(1, ncores * nseq, WP, 8, 64)
    ncs = stk("ncs")[None]; nks = stk("nks").reshape(1, ncores, 512, 8, 64); nvs = stk("nvs").reshape(1, ncores, 512, 8, 64)
    return tuple(np.ascontiguousarray(a, dtype=np.float32) for a in (yp, ys, ncp, nkp, nvp, ncs, nks, nvs))


def kernel(**inputs):
    return run(inputs, 8, 2, 4096)
```
